# Optimizing a Trainium2 kernel written in Bass

```python
import math
import jax, jax.numpy as jnp
from jax import lax
import numpy as np

D_MODEL = 1024
BATCH = 32
SEQ = 2048
DEPTH = 2

N_MIXERS = 2
RMS_EPS = 1e-6

MLA_HEADS = 8
QK_NOPE_DIM = 128
QK_ROPE_DIM = 64
V_HEAD_DIM = 128
Q_LORA_RANK = 384
KV_LORA_RANK = 256
ROPE_THETA = 10000.0
Q_BLOCK = 128
MLA_WIDTH = MLA_HEADS * V_HEAD_DIM
MLA_QK_DIM = QK_NOPE_DIM + QK_ROPE_DIM
MLA_IN_WIDTH = Q_LORA_RANK + KV_LORA_RANK + QK_ROPE_DIM + MLA_WIDTH

HYENA_WIDTH = D_MODEL
HYENA_ORDER = 2
SHORT_CONV = 3
POS_EMB_DIM = 33
POS_BANDS = (POS_EMB_DIM - 1) // 2
FILTER_HIDDEN = 64
N_DIRS = 2
DECAY_TARGET = 1e-2
FAST_DECAY_PCT = 0.3
SLOW_DECAY_PCT = 1.5
MIN_DECAY = math.log(DECAY_TARGET) / SLOW_DECAY_PCT
MAX_DECAY = math.log(DECAY_TARGET) / FAST_DECAY_PCT
FILTER_OUT_GAIN = 0.05
HYENA_IN_WIDTH = (HYENA_ORDER + 1) * HYENA_WIDTH + HYENA_WIDTH

kernel_name = "mla_hyena_interleaved_encoder"


def rms_norm(x, g):
    xf = x.astype(jnp.float32)
    y = xf * lax.rsqrt(jnp.mean(xf * xf, axis=-1, keepdims=True) + RMS_EPS)
    return (y * g.astype(jnp.float32)).astype(x.dtype)


def rope_tables(L):
    inv = 1.0 / (ROPE_THETA ** (jnp.arange(0, QK_ROPE_DIM, 2, dtype=jnp.float32) / QK_ROPE_DIM))
    ang = jnp.arange(L, dtype=jnp.float32)[:, None] * inv[None, :]
    return jnp.cos(ang), jnp.sin(ang)


def apply_rope(x, cos, sin):
    xf = x.astype(jnp.float32)
    x1, x2 = xf[..., :QK_ROPE_DIM // 2], xf[..., QK_ROPE_DIM // 2:]
    out = jnp.concatenate([x1 * cos - x2 * sin, x1 * sin + x2 * cos], axis=-1)
    return out.astype(x.dtype)


def mla_mixer(h, w_in, q_norm, w_uq, kv_norm, w_ukv, w_out):
    B, L, _ = h.shape
    proj = h @ w_in
    s1 = Q_LORA_RANK
    s2 = s1 + KV_LORA_RANK
    s3 = s2 + QK_ROPE_DIM
    c_q, c_kv, k_pe, gate = proj[..., :s1], proj[..., s1:s2], proj[..., s2:s3], proj[..., s3:]
    q = (rms_norm(c_q, q_norm) @ w_uq).reshape(B, L, MLA_HEADS, MLA_QK_DIM)
    q_nope, q_pe = q[..., :QK_NOPE_DIM], q[..., QK_NOPE_DIM:]
    kv = (rms_norm(c_kv, kv_norm) @ w_ukv).reshape(B, L, MLA_HEADS, QK_NOPE_DIM + V_HEAD_DIM)
    k_nope, v = kv[..., :QK_NOPE_DIM], kv[..., QK_NOPE_DIM:]
    cos, sin = rope_tables(L)
    q_pe = apply_rope(q_pe, cos[:, None, :], sin[:, None, :])
    k_pe = apply_rope(k_pe, cos, sin)
    scale = MLA_QK_DIM ** -0.5
    n_blk = L // Q_BLOCK

    def to_blocks(t):
        return jnp.moveaxis(t.reshape(B, n_blk, Q_BLOCK, *t.shape[2:]), 1, 0)

    def attend(blk):
        qn, qp = blk
        s = (jnp.einsum('bqhd,bkhd->bhqk', qn, k_nope)
             + jnp.einsum('bqhr,bkr->bhqk', qp, k_pe))
        p = jax.nn.softmax(s.astype(jnp.float32) * scale, axis=-1).astype(v.dtype)
        return jnp.einsum('bhqk,bkhd->bqhd', p, v)

    o = lax.map(attend, (to_blocks(q_nope), to_blocks(q_pe)))
    o = jnp.moveaxis(o, 0, 1).reshape(B, L, MLA_WIDTH)
    return (o * jax.nn.silu(gate)) @ w_out


def hyena_filters(L, w1, b1, w2, b2, w3, b3, w4, freq):
    f32 = jnp.float32
    t = jnp.linspace(0.0, 1.0, L, dtype=f32)[:, None]
    w = 2.0 * math.pi * jnp.arange(L, dtype=f32) / L
    bands = jnp.linspace(1e-4, POS_BANDS - 1, POS_BANDS, dtype=f32)
    fw = w[:, None] * bands[None, :]
    z = jnp.concatenate([t, jnp.cos(fw), -jnp.sin(fw)], axis=-1)
    fr = freq.astype(f32)
    a = jnp.sin(fr * (z @ w1.astype(f32) + b1.astype(f32)))
    a = jnp.sin(fr * (a @ w2.astype(f32) + b2.astype(f32)))
    a = jnp.sin(fr * (a @ w3.astype(f32) + b3.astype(f32)))
    hf = (a @ w4.astype(f32)).reshape(L, HYENA_ORDER, N_DIRS, HYENA_WIDTH)
    deltas = jnp.abs(jnp.linspace(MIN_DECAY, MAX_DECAY, HYENA_WIDTH, dtype=f32))
    decay = jnp.exp(-t * deltas[None, :])
    return hf * decay[:, None, None, :]


def two_sided_kernel(h_fwd, h_bwd):
    return jnp.concatenate([h_fwd, jnp.zeros_like(h_fwd[:1]), h_bwd[:0:-1]], axis=0)


def bidir_long_conv(u, k, bias):
    L = u.shape[1]
    uf = u.astype(jnp.float32)
    spec = jnp.fft.rfft(uf, n=2 * L, axis=1) * jnp.fft.rfft(k, n=2 * L, axis=0)[None]
    y = jnp.fft.irfft(spec, n=2 * L, axis=1)[:, :L]
    return (y + uf * bias.astype(jnp.float32)).astype(u.dtype)


def hyena_mixer(h, w_in, conv_w, conv_b, filt_w1, filt_b1, filt_w2, filt_b2,
                filt_w3, filt_b3, filt_w4, filt_freq, filt_bias, w_out):
    B, L, _ = h.shape
    proj = h @ w_in
    u, gate = proj[..., :(HYENA_ORDER + 1) * HYENA_WIDTH], proj[..., (HYENA_ORDER + 1) * HYENA_WIDTH:]
    pad = SHORT_CONV // 2
    up = jnp.pad(u, ((0, 0), (pad, pad), (0, 0)))
    u = conv_b + sum(up[:, j:j + L] * conv_w[j] for j in range(SHORT_CONV))
    x1 = u[..., :HYENA_WIDTH]
    x2 = u[..., HYENA_WIDTH:2 * HYENA_WIDTH]
    v = u[..., 2 * HYENA_WIDTH:]
    filt = hyena_filters(L, filt_w1, filt_b1, filt_w2, filt_b2, filt_w3, filt_b3, filt_w4, filt_freq)
    z = v
    for n, g_n in enumerate((x1, x2)):
        k = two_sided_kernel(filt[:, n, 0], filt[:, n, 1])
        z = g_n * bidir_long_conv(z, k, filt_bias[n])
    return (z * jax.nn.silu(gate)) @ w_out


def setup_inputs(seed: int = 0) -> dict:
    key = jax.random.key(seed)
    ks = iter(jax.random.split(key, 32))
    f32 = jnp.float32

    def nrm(shape, scale):
        return jax.random.normal(next(ks), shape, f32) * scale

    def gain(n):
        return jnp.ones((n,), f32) + nrm((n,), 0.01)

    d = D_MODEL
    return {
        "x": nrm((BATCH, SEQ, d), 1.0),
        "l0_norm": gain(d),
        "l0_w_in": nrm((d, MLA_IN_WIDTH), d ** -0.5),
        "l0_q_norm": gain(Q_LORA_RANK),
        "l0_w_uq": nrm((Q_LORA_RANK, MLA_HEADS * MLA_QK_DIM), Q_LORA_RANK ** -0.5),
        "l0_kv_norm": gain(KV_LORA_RANK),
        "l0_w_ukv": nrm((KV_LORA_RANK, MLA_HEADS * (QK_NOPE_DIM + V_HEAD_DIM)), KV_LORA_RANK ** -0.5),
        "l0_w_out": nrm((MLA_WIDTH, d), MLA_WIDTH ** -0.5),
        "l1_norm": gain(d),
        "l1_w_in": nrm((d, HYENA_IN_WIDTH), d ** -0.5),
        "l1_conv_w": nrm((SHORT_CONV, (HYENA_ORDER + 1) * HYENA_WIDTH), SHORT_CONV ** -0.5),
        "l1_conv_b": nrm(((HYENA_ORDER + 1) * HYENA_WIDTH,), 0.02),
        "l1_filt_w1": nrm((POS_EMB_DIM, FILTER_HIDDEN), POS_EMB_DIM ** -0.5),
        "l1_filt_b1": nrm((FILTER_HIDDEN,), 0.1),
        "l1_filt_w2": nrm((FILTER_HIDDEN, FILTER_HIDDEN), FILTER_HIDDEN ** -0.5),
        "l1_filt_b2": nrm((FILTER_HIDDEN,), 0.1),
        "l1_filt_w3": nrm((FILTER_HIDDEN, FILTER_HIDDEN), FILTER_HIDDEN ** -0.5),
        "l1_filt_b3": nrm((FILTER_HIDDEN,), 0.1),
        "l1_filt_w4": nrm((FILTER_HIDDEN, HYENA_ORDER * N_DIRS * HYENA_WIDTH), FILTER_OUT_GAIN * FILTER_HIDDEN ** -0.5),
        "l1_filt_freq": jnp.ones((FILTER_HIDDEN,), f32) + nrm((FILTER_HIDDEN,), 0.1),
        "l1_filt_bias": nrm((HYENA_ORDER, HYENA_WIDTH), 0.2),
        "l1_w_out": nrm((HYENA_WIDTH, d), HYENA_WIDTH ** -0.5),
        "final_norm": gain(d),
    }


def reference(x, l0_norm, l0_w_in, l0_q_norm, l0_w_uq, l0_kv_norm, l0_w_ukv, l0_w_out,
              l1_norm, l1_w_in, l1_conv_w, l1_conv_b, l1_filt_w1, l1_filt_b1, l1_filt_w2,
              l1_filt_b2, l1_filt_w3, l1_filt_b3, l1_filt_w4, l1_filt_freq, l1_filt_bias,
              l1_w_out, final_norm):
    mixers = (mla_mixer, hyena_mixer)
    layers = (
        (l0_norm, (l0_w_in, l0_q_norm, l0_w_uq, l0_kv_norm, l0_w_ukv, l0_w_out)),
        (l1_norm, (l1_w_in, l1_conv_w, l1_conv_b, l1_filt_w1, l1_filt_b1, l1_filt_w2, l1_filt_b2,
                   l1_filt_w3, l1_filt_b3, l1_filt_w4, l1_filt_freq, l1_filt_bias, l1_w_out)),
    )
    h = x
    for i in range(DEPTH):
        g, params = layers[i]
        h = h + mixers[i % N_MIXERS](rms_norm(h, g), *params)
    return rms_norm(h, final_norm)
```

```python
import math
from contextlib import ExitStack

import numpy as np
import ml_dtypes
import concourse.bass as bass
import concourse.mybir as mybir
from concourse.bass_utils import run_bass_kernel_spmd

F32 = mybir.dt.float32
BF16 = mybir.dt.bfloat16
AF = mybir.ActivationFunctionType
ALU = mybir.AluOpType

T = 2048
D = 1024
NT = 16
NCORES = 8
SEQ_PER_CORE = 4
EPS = 1e-6
NFFT = 4096
SCALE = 192 ** -0.5


class TR:
    ENG = ("pe", "act", "dve", "pool", "sp")
    CE = ("pe", "act", "dve", "pool")

    def __init__(self, nc, es, nds=24):
        self.nc = nc
        self.sem = {e: es.enter_context(nc.semaphore("s_" + e)) for e in self.CE}
        self.base = {e: 0 for e in self.CE}
        self.nidx = {e: 0 for e in self.CE}
        self.ops = {e: [] for e in self.ENG}
        self.seen = {e: {} for e in self.ENG}
        self.lastw = {}
        self.readers = {}
        self.nds = nds
        self.dsem = {q: [es.enter_context(nc.semaphore("d_%s%d" % (q, i))) for i in range(nds)] for q in ("sp", "pool", "act")}
        self.dval = {q: [0] * nds for q in self.dsem}
        self.dn = {q: 0 for q in self.dsem}

    def _deps(self, e, r, w):
        evs = []
        for k in r:
            if k in self.lastw:
                evs.append(self.lastw[k])
        for k in w:
            if k in self.lastw:
                evs.append(self.lastw[k])
            evs.extend(self.readers.get(k, {}).values())
        out = []
        seen = self.seen[e]
        for (name, sem, val) in evs:
            if name == "pe" and e == "pe":
                continue
            if seen.get(name, 0) >= val:
                continue
            seen[name] = val
            out.append((name, sem, val))
        return out

    def _commit(self, e, ev, r, w):
        for k in r:
            self.readers.setdefault(k, {})[ev[0]] = ev
        for k in w:
            self.lastw[k] = ev
            self.readers[k] = {}

    def op(self, e, fn, r=(), w=(), inc=True):
        waits = self._deps(e, r, w)
        self.nidx[e] += 1
        ev = (e, None, self.nidx[e])
        self.ops[e].append([waits, fn, ("ce", e, self.nidx[e])])
        self._commit(e, ev, r, w)

    def dma(self, q, out, in_, r=(), w=(), slow=False):
        i = self.dn[q] % self.nds
        self.dn[q] += 1
        sem = self.dsem[q][i]
        waits = self._deps(q, r, w)
        name = "d_%s%d" % (q, i)
        if self.dval[q][i] > 0 and self.seen[q].get(name, 0) < self.dval[q][i]:
            waits.append((name, sem, self.dval[q][i]))
            self.seen[q][name] = self.dval[q][i]
        self.dval[q][i] += 16
        ev = (name, sem, self.dval[q][i])
        if slow:
            fn = lambda eng, o=out, a=in_: eng.dma_start(out=o, in_=a, allow_slow_non_contiguous=True)
        else:
            fn = lambda eng, o=out, a=in_: eng.dma_start(out=o, in_=a)
        if q in self.CE:
            self.nidx[q] += 1
        self.ops[q].append([waits, fn, ("dma", sem, 16)])
        self._commit(q, ev, r, w)

    def mm(self, out, lhsT, rhs, start, stop, r=(), w=(), inc=None):
        self.op("pe", lambda e: e.matmul(out, lhsT, rhs, start=start, stop=stop), r, w)

    def act(self, out, in_, func, r=(), w=(), scale=1.0, bias=None, accum=None):
        def fn(e):
            kw = {}
            if bias is not None:
                kw["bias"] = bias
            if accum is not None:
                kw["accum_out"] = accum
            return e.activation(out=out, in_=in_, func=func, scale=scale, **kw)
        self.op("act", fn, r, w)

    def tt(self, eng, out, in0, in1, op, r=(), w=()):
        self.op(eng, lambda e: e.tensor_tensor(out=out, in0=in0, in1=in1, op=op), r, w)

    def ts(self, eng, out, in0, s1, s2, op0, op1=None, r=(), w=()):
        if op1 is None:
            self.op(eng, lambda e: e.tensor_scalar(out=out, in0=in0, scalar1=s1, scalar2=None, op0=op0), r, w)
        else:
            self.op(eng, lambda e: e.tensor_scalar(out=out, in0=in0, scalar1=s1, scalar2=s2, op0=op0, op1=op1), r, w)

    def stt(self, eng, out, in0, scalar, in1, op0, op1, r=(), w=()):
        self.op(eng, lambda e: e.scalar_tensor_tensor(out=out, in0=in0, scalar=scalar, in1=in1, op0=op0, op1=op1), r, w)

    def copy(self, eng, out, in_, r=(), w=()):
        if eng == "act":
            self.op("act", lambda e: e.activation(out=out, in_=in_, func=AF.Copy), r, w)
        else:
            self.op(eng, lambda e: e.tensor_copy(out=out, in_=in_), r, w)

    def recip(self, out, in_, r=(), w=()):
        self.op("dve", lambda e: e.reciprocal(out=out, in_=in_), r, w)

    def memset(self, eng, ap, val, w=()):
        self.op(eng, lambda e: e.memset(ap, val), (), w)

    def flush(self, final=False):
        nc = self.nc
        last = {}
        for f in self.CE:
            li = 0
            for waits, fn, tag in self.ops[f]:
                if tag[0] == "ce":
                    li = tag[2]
            last[f] = li
        for e in self.ENG:
            waits = []
            for f in self.CE:
                if f != e and last[f] > self.seen[e].get(f, 0):
                    waits.append((f, None, last[f]))
                    self.seen[e][f] = last[f]
            for q in self.dsem:
                for i in range(self.nds):
                    name = "d_%s%d" % (q, i)
                    if self.dval[q][i] > self.seen[e].get(name, 0):
                        waits.append((name, self.dsem[q][i], self.dval[q][i]))
                        self.seen[e][name] = self.dval[q][i]
            self.ops[e].append([waits, None, None])
        needed = {f: set() for f in self.CE}
        for e in self.ENG:
            for waits, fn, tag in self.ops[e]:
                for (name, sem, val) in waits:
                    if name in needed:
                        needed[name].add(val)
        rank = {}
        for f in self.CE:
            srt = sorted(needed[f])
            rank[f] = {idx: self.base[f] + k + 1 for k, idx in enumerate(srt)}
            self.base[f] += len(srt)
        ops = self.ops
        self.ops = {e: [] for e in self.ENG}
        self.lastw = {}
        self.readers = {}
        sems = self.sem

        def replay(eng, lst):
            for waits, fn, tag in lst:
                for (name, sem, val) in waits:
                    if name in rank:
                        eng.wait_ge(sems[name], rank[name][val])
                    else:
                        eng.wait_ge(sem, val)
                if fn is not None:
                    ins = fn(eng)
                    if tag[0] == "dma":
                        ins.then_inc(tag[1], tag[2])
                    elif tag[2] in rank[tag[1]]:
                        ins.then_inc(sems[tag[1]], 1)

        with nc.Block() as block:
            @block.tensor
            def _(e):
                replay(e, ops["pe"])

            @block.scalar
            def _(e):
                replay(e, ops["act"])

            @block.vector
            def _(e):
                replay(e, ops["dve"])

            @block.gpsimd
            def _(e):
                replay(e, ops["pool"])

            @block.sync
            def _(e):
                replay(e, ops["sp"])


_CONSTS = None


def _consts():
    global _CONSTS
    if _CONSTS is not None:
        return _CONSTS
    bf = ml_dtypes.bfloat16
    idx = np.arange(2048, dtype=np.float64) + 0.5
    ang = 2.0 * np.pi * np.outer(idx, idx) / NFFT
    c2 = np.cos(ang)
    s2 = np.sin(ang)

    def tile_lhsT(m):
        return np.ascontiguousarray(m.reshape(16, 128, 16, 128).transpose(2, 1, 0, 3)).astype(bf)

    c2f = tile_lhsT(c2)
    s2f = tile_lhsT(s2)
    d = np.arange(2048, dtype=np.float64)
    angm = 2.0 * np.pi * np.outer(d, idx) / NFFT
    cmt = tile_lhsT(np.cos(angm))
    smt = tile_lhsT(np.sin(angm))
    f32 = np.float32
    t = np.linspace(0.0, 1.0, T, dtype=f32)[:, None]
    min_decay = math.log(1e-2) / 1.5
    max_decay = math.log(1e-2) / 0.3
    deltas = np.abs(np.linspace(min_decay, max_decay, 1024, dtype=f32))
    decay = np.exp(-t * deltas[None, :]).astype(f32)
    w = (f32(2.0 * math.pi) * np.arange(T, dtype=f32) / f32(T)).astype(f32)
    bands = np.linspace(1e-4, 15, 16, dtype=f32)
    fw = (w[:, None] * bands[None, :]).astype(f32)
    z = np.concatenate([t, np.cos(fw), -np.sin(fw)], axis=-1).astype(f32)
    zt = np.ascontiguousarray(z.T)
    inv = (1.0 / (f32(10000.0) ** (np.arange(0, 64, 2, dtype=f32) / f32(64)))).astype(f32)
    angr = (np.arange(T, dtype=f32)[:, None] * inv[None, :]).astype(f32)
    cos = np.cos(angr).T.astype(f32)
    sin = np.sin(angr).T.astype(f32)
    cos2 = np.ascontiguousarray(np.concatenate([cos, cos, cos, cos], axis=0))
    sin2 = np.ascontiguousarray(np.concatenate([-sin, sin, -sin, sin], axis=0))
    ident = np.eye(128, dtype=np.float32).astype(bf)
    _CONSTS = dict(c_ident=ident, c_c2f=c2f, c_s2f=s2f, c_cmt=cmt, c_smt=smt, c_decay=decay,
                   c_zt=zt, c_cos2=cos2, c_sin2=sin2)
    return _CONSTS


PARAMS = [
    ("l0_norm", [1024]), ("l0_w_in", [1024, 1728]), ("l0_q_norm", [384]), ("l0_w_uq", [384, 1536]),
    ("l0_kv_norm", [256]), ("l0_w_ukv", [256, 2048]), ("l0_w_out", [1024, 1024]),
    ("l1_norm", [1024]), ("l1_w_in", [1024, 4096]), ("l1_conv_w", [3, 3072]), ("l1_conv_b", [3072]),
    ("l1_filt_w1", [33, 64]), ("l1_filt_b1", [64]), ("l1_filt_w2", [64, 64]), ("l1_filt_b2", [64]),
    ("l1_filt_w3", [64, 64]), ("l1_filt_b3", [64]), ("l1_filt_w4", [64, 4096]), ("l1_filt_freq", [64]),
    ("l1_filt_bias", [2, 1024]), ("l1_w_out", [1024, 1024]), ("final_norm", [1024]),
]


def build_program(nseq=SEQ_PER_CORE, dbg=False):
    nc = bass.Bass("TRN2", target_bir_lowering=False)
    H = {}
    H["x"] = nc.dram_tensor("x", [nseq, T, D], F32, kind="ExternalInput")
    for name, shape in PARAMS:
        H[name] = nc.dram_tensor(name, shape, F32, kind="ExternalInput")
    cs = _consts()
    for name, arr in cs.items():
        H[name] = nc.dram_tensor(name, list(arr.shape), BF16 if arr.dtype != np.float32 else F32, kind="ExternalInput")
    out_h = nc.dram_tensor("out", [nseq, T, D], F32, kind="ExternalOutput")
    A = {k: v.ap() for k, v in H.items()}
    out_ap = out_h.ap()

    def scratch(name, shape, dt):
        return nc.dram_tensor(name, shape, dt, kind="ExternalOutput" if (dbg and name in ("h1s",)) else "Internal").ap()

    w0s = scratch("w0s", [128, 8, 1920], BF16)
    wo0s = scratch("wo0s", [128, 8, 1024], BF16)
    w1s = scratch("w1s", [128, 8, 4096], BF16)
    wo1s = scratch("wo1s", [128, 8, 1024], BF16)
    kfs = scratch("kfs", [2, 16, 128, 2, 1024], F32)
    h1s = scratch("h1s", [nseq, T, D], F32)
    x1t = scratch("x1t", [nseq, T, D], BF16)
    xgs = scratch("xgs", [nseq, D, T], BF16)
    wqn_s = scratch("wqn_s", [128, 3, 1024], BF16)
    wqp_s = scratch("wqp_s", [128, 3, 512], BF16)
    wqps_s = scratch("wqps_s", [128, 3, 512], BF16)
    wkn_s = scratch("wkn_s", [128, 2, 1024], BF16)
    wv_s = scratch("wv_s", [128, 2, 1024], BF16)

    _uq = [0]

    def uq(name):
        _uq[0] += 1
        return "%s_%d" % (name, _uq[0])

    with ExitStack() as es:
        tr = TR(nc, es)
        sb = lambda name, shape, dt: es.enter_context(nc.sbuf_tensor(uq(name), shape, dt))
        ps = es.enter_context(nc.psum_tensor("ps", [128, 4096], F32))

        def bank(b, n=1):
            return ps[:, b * 512:(b + n) * 512]

        def pk(b, n=1):
            return [("ps", b + i) for i in range(n)]

        ident = sb("ident", [128, 128], BF16)
        ones = sb("ones", [128, 128], BF16)
        epst = sb("epst", [128, 1], F32)
        onesf = sb("onesf", [128, 128], F32)
        gfin = sb("gfin", [128, D], F32)
        convw = sb("convw", [128, 3, 24], F32)
        convb = sb("convb", [128, 24], F32)
        HNT = None

        with ExitStack() as pes:
            psb = lambda name, shape, dt: pes.enter_context(nc.sbuf_tensor(uq(name), shape, dt))
            tr.dma("sp", ident[:], A["c_ident"], w=["ident"])
            wqn = psb("wqn", [128, 3, 1024], BF16)
            wqp = psb("wqp", [128, 3, 512], BF16)
            wqps = psb("wqps", [128, 3, 512], BF16)
            wkn = psb("wkn", [128, 2, 1024], BF16)
            wv = psb("wv", [128, 2, 1024], BF16)
            tr.memset("pool", ones[:], 1.0, w=["ones"])
            tr.memset("pool", epst[:], EPS, w=["eps"])
            tr.memset("pool", onesf[:], 1.0, w=["onesf"])
            tr.dma("sp", gfin[:], bass.AP(H["final_norm"], 0, [[0, 128], [1, D]]), w=["gfin"])
            for j in range(3):
                tr.dma("sp", convw[:, j, :], A["l1_conv_w"][j, :].rearrange("(c p) -> p c", p=128), w=[("convw", j)], slow=True)
            tr.dma("sp", convb[:], A["l1_conv_b"].rearrange("(c p) -> p c", p=128), w=["convb"], slow=True)
            g0 = psb("g0", [128, 8], F32)
            g1 = psb("g1", [128, 8], F32)
            gq = psb("gq", [128, 3], F32)
            gkv = psb("gkv", [128, 2], F32)
            tr.dma("sp", g0[:], A["l0_norm"].rearrange("(c p) -> p c", p=128), w=["g0"], slow=True)
            tr.dma("sp", g1[:], A["l1_norm"].rearrange("(c p) -> p c", p=128), w=["g1"], slow=True)
            tr.dma("sp", gq[:], A["l0_q_norm"].rearrange("(c p) -> p c", p=128), w=["gq"], slow=True)
            tr.dma("sp", gkv[:], A["l0_kv_norm"].rearrange("(c p) -> p c", p=128), w=["gkv"], slow=True)

            stg = [psb("stg%d" % i, [128, 2048], F32) for i in range(2)]
            wbt = [psb("wbt%d" % i, [128, 2048], BF16) for i in range(2)]
            cvt_n = [0]

            p1_tasks = []

            def convert(src_ap, ncols, gain, gkey, consume):
                p1_tasks.append(lambda: convert_now(src_ap, ncols, gain, gkey, consume))

            def convert_now(src_ap, ncols, gain, gkey, consume):
                i = cvt_n[0] % 2
                cvt_n[0] += 1
                tr.dma("sp", stg[i][:, 0:ncols], src_ap, w=[("stg", i)])
                if gain is None:
                    tr.copy("act", wbt[i][:, 0:ncols], stg[i][:, 0:ncols], r=[("stg", i)], w=[("wbt", i)])
                else:
                    tr.act(wbt[i][:, 0:ncols], stg[i][:, 0:ncols], AF.Identity, scale=gain, r=[("stg", i), gkey], w=[("wbt", i)])
                consume(wbt[i], ("wbt", i))

            def w0_task(kc):
                i = cvt_n[0] % 2
                cvt_n[0] += 1
                tr.dma("sp", stg[i][:, 0:1728], A["l0_w_in"][kc * 128:(kc + 1) * 128, :], w=[("stg", i)])
                gsc = g0[:, kc:kc + 1]
                pieces = [(0, 640, 0), (640, 64, 640), (704, 64, 640), (768, 32, 672), (800, 32, 640), (832, 32, 672), (864, 32, 640),
                          (896, 1024, 704)]
                for pi, (dst, n_, src) in enumerate(pieces):
                    tr.act(wbt[i][:, dst:dst + n_], stg[i][:, src:src + n_], AF.Identity, scale=gsc,
                           r=[("stg", i), "g0"], w=[("wbt", i)])
                tr.dma("act", w0s[:, kc, :], wbt[i][:, 0:1920], r=[("wbt", i)], w=["w0s"])
            for kc in range(8):
                p1_tasks.append(lambda kc=kc: w0_task(kc))
            for kc in range(8):
                convert(A["l0_w_out"][kc * 128:(kc + 1) * 128, :], 1024, None, None,
                        lambda wb, key, kc=kc: tr.dma("act", wo0s[:, kc, :], wb[:, 0:1024], r=[key], w=["wo0s"]))
                convert(A["l1_w_out"][kc * 128:(kc + 1) * 128, :], 1024, None, None,
                        lambda wb, key, kc=kc: tr.dma("act", wo1s[:, kc, :], wb[:, 0:1024], r=[key], w=["wo1s"]))
                for hf in range(2):
                    convert(A["l1_w_in"][kc * 128:(kc + 1) * 128, hf * 2048:(hf + 1) * 2048], 2048, g1[:, kc:kc + 1], "g1",
                            lambda wb, key, kc=kc, hf=hf: tr.dma(
                                "act", w1s[:, kc, hf * 2048:(hf + 1) * 2048], wb[:, 0:2048], r=[key], w=["w1s"]))
            for kc in range(3):
                def consq(wb, key, kc=kc):
                    v = wb[:, 0:1536].rearrange("p (h n) -> p h n", n=192)
                    tr.copy("dve", wqn[:, kc, :].rearrange("p (h n) -> p h n", n=128), v[:, :, 0:128], r=[key], w=["wqn"])
                    tr.copy("dve", wqp[:, kc, :].rearrange("p (h n) -> p h n", n=64), v[:, :, 128:192], r=[key], w=["wqp"])
                    o = wqps[:, kc, :].rearrange("p (h n) -> p h n", n=64)
                    tr.copy("dve", o[:, :, 0:32], v[:, :, 160:192], r=[key], w=["wqps"])
                    tr.copy("dve", o[:, :, 32:64], v[:, :, 128:160], r=[key], w=["wqps"])
                convert(A["l0_w_uq"][kc * 128:(kc + 1) * 128, :], 1536, gq[:, kc:kc + 1], "gq", consq)
            for kc in range(2):
                def conskv(wb, key, kc=kc):
                    v = wb[:, 0:2048].rearrange("p (h n) -> p h n", n=256)
                    tr.copy("dve", wkn[:, kc, :].rearrange("p (h n) -> p h n", n=128), v[:, :, 0:128], r=[key], w=["wkn"])
                    tr.copy("dve", wv[:, kc, :].rearrange("p (h n) -> p h n", n=128), v[:, :, 128:256], r=[key], w=["wv"])
                convert(A["l0_w_ukv"][kc * 128:(kc + 1) * 128, :], 2048, gkv[:, kc:kc + 1], "gkv", conskv)

            def pop_p1():
                if p1_tasks:
                    p1_tasks.pop(0)()

            pes0 = pes
            psb0 = lambda name, shape, dt: pes0.enter_context(nc.sbuf_tensor(uq(name), shape, dt))
            a3b = psb0("a3b", [64, T], BF16)
            w4b = psb0("w4b", [64, 4096], BF16)
            fbias = psb0("fbias", [1, 2, 1024], F32)
            with ExitStack() as pes:
              psb = lambda name, shape, dt: pes.enter_context(nc.sbuf_tensor(uq(name), shape, dt))
              stgw = [psb("stgw%d" % i, [64, 2048], F32) for i in range(2)]
              fw1 = psb("fw1", [33, 64], F32)
              fw2 = psb("fw2", [64, 64], F32)
              fw3 = psb("fw3", [64, 64], F32)
              fb = psb("fb", [64, 3], F32)
              ffr = psb("ffr", [64, 1], F32)
              ffb = psb("ffb", [64, 3], F32)
              zt = psb("zt", [33, T], F32)
              aa = [psb("aa%d" % i, [64, T], F32) for i in range(2)]
              m1 = psb("m1", [64, T], F32)
              m2 = psb("m2", [64, T], F32)
              tr.dma("sp", fw1[:], A["l1_filt_w1"], w=["fw1"])
              tr.dma("sp", fw2[:], A["l1_filt_w2"], w=["fw2"])
              tr.dma("sp", fw3[:], A["l1_filt_w3"], w=["fw3"])
              for i, nm in enumerate(("l1_filt_b1", "l1_filt_b2", "l1_filt_b3")):
                  tr.dma("sp", fb[:, i:i + 1], A[nm].rearrange("(p o) -> p o", o=1), w=["fb"], slow=True)
              tr.dma("sp", ffr[:], A["l1_filt_freq"].rearrange("(p o) -> p o", o=1), w=["ffr"], slow=True)
              tr.dma("sp", zt[:], A["c_zt"], w=["zt"])
              tr.dma("sp", fbias[:], A["l1_filt_bias"].rearrange("(o a) c -> o a c", o=1), w=["fbias"])
              for hf in range(2):
                  tr.dma("sp", stgw[hf][0:64, :], A["l1_filt_w4"][:, hf * 2048:(hf + 1) * 2048], w=[("stgw", hf)])
                  tr.copy("dve", w4b[:, hf * 2048:(hf + 1) * 2048], stgw[hf][0:64, :], r=[("stgw", hf)], w=["w4b"])
              tr.ts("dve", ffb[:], fb[:], ffr[:, 0:1], None, ALU.mult, r=["fb", "ffr"], w=["ffb"])
              PI = float(np.pi)
              for layer in range(3):
                  lw = (fw1, fw2, fw3)[layer]
                  src = zt if layer == 0 else aa[(layer - 1) % 2]
                  srck = "zt" if layer == 0 else ("aa", (layer - 1) % 2)
                  kdim = 33 if layer == 0 else 64
                  for tb in range(4):
                      tr.mm(ps[0:64, tb * 512:(tb + 1) * 512], lw[0:kdim, :], src[0:kdim, tb * 512:(tb + 1) * 512], True, True,
                            r=[srck, "fw%d" % (layer + 1)], w=pk(tb))
                  dst = aa[layer % 2]
                  dk = ("aa", layer % 2)
                  tr.ts("dve", dst[:], ps[0:64, 0:T], ffr[:, 0:1], ffb[:, layer:layer + 1], ALU.mult, ALU.add,
                        r=pk(0, 4) + ["ffr", "ffb"], w=[dk])
                  tr.ts("dve", m1[:], dst[:], PI, -2.0 * PI, ALU.is_gt, ALU.mult, r=[dk], w=["m1"])
                  tr.ts("dve", m2[:], dst[:], -PI, 2.0 * PI, ALU.is_lt, ALU.mult, r=[dk], w=["m2"])
                  tr.tt("dve", dst[:], dst[:], m1[:], ALU.add, r=[dk, "m1"], w=[dk])
                  tr.tt("dve", dst[:], dst[:], m2[:], ALU.add, r=[dk, "m2"], w=[dk])
                  if layer < 2:
                      tr.act(dst[:], dst[:], AF.Sin, r=[dk], w=[dk])
                  else:
                      tr.act(a3b[:], dst[:], AF.Sin, r=[dk], w=["a3b"])

              tr.flush()
            with ExitStack() as pes:
              psb = lambda name, shape, dt: pes.enter_context(nc.sbuf_tensor(uq(name), shape, dt))
              ksum = psb("ksum", [128, 16, 1024], BF16)
              kdif = psb("kdif", [128, 16, 1024], BF16)
              dct = [psb("dct%d" % i, [128, 1024], F32) for i in range(2)]
              kft = [psb("kft%d" % i, [128, 1024], F32) for i in range(2)]
              kbt = [psb("kbt%d" % i, [128, 1024], F32) for i in range(2)]
              cmT = [psb("cmT%d" % i, [128, 16, 128], BF16) for i in range(2)]
              smT = [psb("smT%d" % i, [128, 16, 128], BF16) for i in range(2)]
              kout = [psb("kout%d" % i, [128, 2, 512], F32) for i in range(2)]
              n_it = 0
              for order in range(2):
                  for tt_ in range(16):
                      i = n_it % 2
                      n_it += 1
                      pop_p1()
                      tr.dma("sp", dct[i][:], A["c_decay"][tt_ * 128:(tt_ + 1) * 128, :], w=[("dct", i)])
                      for part in range(4):
                          col = order * 2048 + part * 512
                          tr.mm(bank(part), a3b[:, tt_ * 128:(tt_ + 1) * 128], w4b[:, col:col + 512], True, True,
                                r=["a3b", "w4b"], w=pk(part))
                      tr.tt("dve", kft[i][:], ps[:, 0:1024], dct[i][:], ALU.mult, r=pk(0, 2) + [("dct", i)], w=[("kft", i)])
                      tr.tt("dve", kbt[i][:], ps[:, 1024:2048], dct[i][:], ALU.mult, r=pk(2, 2) + [("dct", i)], w=[("kbt", i)])
                      tr.tt("pool", ksum[:, tt_, :], kft[i][:], kbt[i][:], ALU.add, r=[("kft", i), ("kbt", i)], w=["ksum"])
                      tr.tt("pool", kdif[:, tt_, :], kft[i][:], kbt[i][:], ALU.subtract, r=[("kft", i), ("kbt", i)], w=["kdif"])
                      if tt_ == 0:
                          tr.tt("pool", ksum[0:1, 0, :], kft[i][0:1, :], fbias[0:1, order, :], ALU.add,
                                r=[("kft", i), "fbias", "ksum"], w=["ksum"])
                  for j in range(16):
                      i = j % 2
                      pop_p1()
                      tr.dma("sp", cmT[i][:], A["c_cmt"][j], w=[("cmT", i)])
                      tr.dma("sp", smT[i][:], A["c_smt"][j], w=[("smT", i)])
                      for ch in range(2):
                          o = (j * 2 + ch) % 2
                          b0 = 4 + 2 * o
                          for dc in range(16):
                              tr.mm(bank(b0), cmT[i][:, dc, :], ksum[:, dc, ch * 512:(ch + 1) * 512], dc == 0, dc == 15,
                                    r=[("cmT", i), "ksum"], w=pk(b0))
                          for dc in range(16):
                              tr.mm(bank(b0 + 1), smT[i][:, dc, :], kdif[:, dc, ch * 512:(ch + 1) * 512], dc == 0, dc == 15,
                                    r=[("smT", i), "kdif"], w=pk(b0 + 1))
                          tr.act(kout[o][:, 0, :], bank(b0), AF.Identity, scale=2.0 / NFFT, r=pk(b0), w=[("kout", o)])
                          tr.act(kout[o][:, 1, :], bank(b0 + 1), AF.Identity, scale=-2.0 / NFFT, r=pk(b0 + 1), w=[("kout", o)])
                          tr.dma("act", kfs[order, j, :, :, ch * 512:(ch + 1) * 512], kout[o][:], r=[("kout", o)], w=["kfs"])
              while p1_tasks:
                  pop_p1()
              tr.dma("act", wqn_s, wqn[:], r=["wqn"], w=["wqn_s"])
              tr.dma("act", wqp_s, wqp[:], r=["wqp"], w=["wqp_s"])
              tr.dma("act", wqps_s, wqps[:], r=["wqps"], w=["wqps_s"])
              tr.dma("act", wkn_s, wkn[:], r=["wkn"], w=["wkn_s"])
              tr.dma("act", wv_s, wv[:], r=["wv"], w=["wv_s"])
              tr.flush()
        HNT = sb("HNT", [128, 8, T], BF16)
        for b in range(nseq):
            with ExitStack() as les:
                lsb = lambda name, shape, dt: les.enter_context(nc.sbuf_tensor(uq(name), shape, dt))
                cos2 = lsb("cos2", [128, T], F32)
                sin2 = lsb("sin2", [128, T], F32)
                wqn = lsb("wqn", [128, 3, 1024], BF16)
                wqp = lsb("wqp", [128, 3, 512], BF16)
                wqps = lsb("wqps", [128, 3, 512], BF16)
                wkn = lsb("wkn", [128, 2, 1024], BF16)
                wv = lsb("wv", [128, 2, 1024], BF16)
                st = lsb("st", [128, 16], F32)
                CQ = lsb("CQ", [128, 3, T], BF16)
                CKV = lsb("CKV", [128, 2, T], BF16)
                KPE = lsb("KPE", [128, 2, T], BF16)
                GATE = lsb("GATE", [128, 8, T], BF16)
                WO = lsb("WO", [128, 8, 1024], BF16)
                ses = ExitStack()
                ssb = lambda name, shape, dt: ses.enter_context(nc.sbuf_tensor(uq(name), shape, dt))
                XT = [ssb("XT%d" % i, [128, D], F32) for i in range(3)]
                HNR = [ssb("HNR%d" % i, [128, D], BF16) for i in range(3)]
                junk = ssb("junk", [128, D], BF16)
                W0 = [ssb("W0_%d" % i, [128, 8, 512], BF16) for i in range(2)]
                SQ = [ssb("SQ%d" % i, [128, 512], BF16) for i in range(2)]
                RQ = ssb("RQ", [128, T], F32)
                TA = [ssb("TA%d" % i, [128, 512], F32) for i in range(2)]
                TB = [ssb("TB%d" % i, [128, 512], F32) for i in range(2)]

                def norm_part(src, srck, slot):
                    c0 = 3 * slot
                    sk = ("st", slot)
                    tr.act(junk[:], src[:], AF.Square, accum=st[:, c0:c0 + 1], r=[srck], w=[sk])
                    tr.act(st[:, c0 + 1:c0 + 2], st[:, c0:c0 + 1], AF.Sqrt, scale=1.0 / D, bias=epst[:, 0:1], r=[sk, "eps"], w=[sk])
                    tr.recip(st[:, c0 + 2:c0 + 3], st[:, c0 + 1:c0 + 2], r=[sk], w=[sk])
                    tr.ts("dve", HNR[slot][:], src[:], st[:, c0 + 2:c0 + 3], None, ALU.mult, r=[srck, sk], w=[("HNR", slot)])

                def tr_part(i, slot, pb):
                    for c in range(8):
                        tr.mm(ps[:, pb * 512 + c * 128: pb * 512 + (c + 1) * 128], HNR[slot][:, c * 128:(c + 1) * 128], ident[:],
                              True, True, r=[("HNR", slot), "ident"], w=pk(pb, 2))
                    tr.copy("act", HNT[:, :, i * 128:(i + 1) * 128], bank(pb, 2).rearrange("p (c n) -> p c n", n=128),
                            r=pk(pb, 2), w=[("HNT", i)])

                for i in range(NT + 1):
                    if i < NT:
                        sl = i % 3
                        tr.dma("sp", XT[sl][:], A["x"][b, i * 128:(i + 1) * 128, :], w=[("XT", sl)])
                        norm_part(XT[sl], ("XT", sl), sl)
                    if i >= 1:
                        tr_part(i - 1, (i - 1) % 3, 2 * ((i - 1) % 3))
                hnt_all = [("HNT", i) for i in range(NT)]
                tr.dma("sp", cos2[:], A["c_cos2"], w=["cos2"])
                tr.dma("sp", sin2[:], A["c_sin2"], w=["sin2"])
                tr.dma("sp", wqn[:], wqn_s, w=["wqn"])
                tr.dma("sp", wqp[:], wqp_s, w=["wqp"])
                tr.dma("sp", wqps[:], wqps_s, w=["wqps"])
                tr.dma("sp", wkn[:], wkn_s, w=["wkn"])
                tr.dma("sp", wv[:], wv_s, w=["wv"])

                w0_loaded = {}

                def inproj(fc, consume, wsrc, ps_base=0):
                    g = fc // 4
                    s = g % 2
                    if g not in w0_loaded:
                        ncol = min(512, 1920 - g * 512)
                        tr.dma("sp", W0[s][:, :, 0:ncol], w0s[:, :, g * 512:g * 512 + ncol], r=[], w=[("W0", s)])
                        w0_loaded[g] = True
                    off = (fc % 4) * 128
                    for tb in range(4):
                        bk = ps_base + tb
                        for kc in range(8):
                            tr.mm(bank(bk), W0[s][:, kc, off:off + 128], HNT[:, kc, tb * 512:(tb + 1) * 512], kc == 0, kc == 7,
                                  r=[("W0", s)] + hnt_all[tb * 4:(tb + 1) * 4], w=pk(bk))
                        consume(tb, bk)

                def rms_latent(buf, bkey, nch, fc0, width):
                    for c in range(nch):
                        def cons(tb, bk, c=c):
                            tr.copy("act", buf[:, c, tb * 512:(tb + 1) * 512], bank(bk), r=pk(bk), w=[(bkey, c, tb)])
                            s = (c * 4 + tb) % 2
                            tr.act(SQ[s][:], bank(bk), AF.Square, r=pk(bk), w=[("SQ", s)])
                            tr.mm(bank(4 + tb), ones[:], SQ[s][:], c == 0, c == nch - 1, r=["ones", ("SQ", s)], w=pk(4 + tb))
                        inproj(fc0 + c, cons, None)
                    for tb in range(4):
                        tr.act(RQ[:, tb * 512:(tb + 1) * 512], bank(4 + tb), AF.Sqrt, scale=1.0 / width, bias=epst[:, 0:1],
                               r=pk(4 + tb) + ["eps"], w=[("RQ", tb)])
                        tr.recip(RQ[:, tb * 512:(tb + 1) * 512], RQ[:, tb * 512:(tb + 1) * 512], r=[("RQ", tb)], w=[("RQ", tb)])
                        for c in range(nch):
                            tr.tt("dve", buf[:, c, tb * 512:(tb + 1) * 512], buf[:, c, tb * 512:(tb + 1) * 512],
                                  RQ[:, tb * 512:(tb + 1) * 512], ALU.mult, r=[(bkey, c, tb), ("RQ", tb)], w=[(bkey, c, tb)])

                rms_latent(CQ, "CQ", 3, 0, 384.0)
                rms_latent(CKV, "CKV", 2, 3, 256.0)

                def rope_evac(dst, dkey, tbl_done):
                    pass

                def consA(tb, bk):
                    tr.tt("dve", RQ[:, tb * 512:(tb + 1) * 512], bank(bk), cos2[:, tb * 512:(tb + 1) * 512], ALU.mult,
                          r=pk(bk) + ["cos2"], w=[("RQ", tb)])
                inproj(5, consA, None)

                def consB(tb, bk):
                    tr.tt("dve", TB[tb % 2][:], bank(bk), sin2[:, tb * 512:(tb + 1) * 512], ALU.mult,
                          r=pk(bk) + ["sin2"], w=[("TB", tb % 2)])
                    if tb == 0:
                        tr.memset("pool", KPE[64:128, 0, :], 0.0, w=[("KPE", t_) for t_ in range(4)])
                        tr.memset("pool", KPE[0:64, 1, :], 0.0, w=[("KPE", t_) for t_ in range(4)])
                    tr.tt("pool", KPE[0:64, 0, tb * 512:(tb + 1) * 512], RQ[0:64, tb * 512:(tb + 1) * 512], TB[tb % 2][0:64, :], ALU.add,
                          r=[("RQ", tb), ("TB", tb % 2)], w=[("KPE", tb)])
                    tr.tt("pool", KPE[64:128, 1, tb * 512:(tb + 1) * 512], RQ[64:128, tb * 512:(tb + 1) * 512], TB[tb % 2][64:128, :], ALU.add,
                          r=[("RQ", tb), ("TB", tb % 2)], w=[("KPE", tb)])
                inproj(6, consB, None)
                for g in range(8):
                    def consG(tb, bk, g=g):
                        tr.act(GATE[:, g, tb * 512:(tb + 1) * 512], bank(bk), AF.Silu, r=pk(bk), w=[("GATE", g, tb)])
                    inproj(7 + g, consG, None)

                tr.flush()
                ses.close()
                ses = ExitStack()
                QN = [ssb("QN%d" % i, [128, T], BF16) for i in range(2)]
                KN = [ssb("KN%d" % i, [128, T], BF16) for i in range(2)]
                QP = [ssb("QP%d" % i, [128, T], BF16) for i in range(2)]
                VH = [ssb("VH%d" % i, [128, 16, 128], BF16) for i in range(2)]
                NPT = 6
                PT = [ssb("PT%d" % i, [128, 512], BF16) for i in range(NPT)]
                RL = ssb("RL", [128, 512], F32)
                tr.dma("sp", WO[:], wo0s, w=["WO"])
                ACD = [ssb("ACD%d" % i, [128, 512], F32) for i in range(2)]
                ACP = [ssb("ACP%d" % i, [128, 512], F32) for i in range(2)]
                TA = [ssb("TAc%d" % i, [128, 512], F32) for i in range(2)]
                TB = [ssb("TBc%d" % i, [128, 512], F32) for i in range(2)]
                cqk = lambda tb: [("CQ", c, tb) for c in range(3)]
                ckvk = lambda tb: [("CKV", c, tb) for c in range(2)]

                def proj_tasks(h):
                    sl = h % 2
                    tasks = []
                    for tb in range(4):
                        def tq(tb=tb):
                            for kc in range(3):
                                tr.mm(bank(7), wqn[:, kc, h * 128:(h + 1) * 128], CQ[:, kc, tb * 512:(tb + 1) * 512], kc == 0, kc == 2,
                                      r=["wqn"] + cqk(tb), w=pk(7))
                            tr.copy("act", QN[sl][:, tb * 512:(tb + 1) * 512], bank(7), r=pk(7), w=[("QN", sl, tb)])
                        tasks.append(tq)

                        def tk(tb=tb):
                            for kc in range(2):
                                tr.mm(bank(7), wkn[:, kc, h * 128:(h + 1) * 128], CKV[:, kc, tb * 512:(tb + 1) * 512], kc == 0, kc == 1,
                                      r=["wkn"] + ckvk(tb), w=pk(7))
                            tr.copy("act", KN[sl][:, tb * 512:(tb + 1) * 512], bank(7), r=pk(7), w=[("KN", sl, tb)])
                        tasks.append(tk)
                        if h % 2 == 0:
                            hp = h // 2
                            psl = hp % 2

                            def tpa(tb=tb):
                                for kc in range(3):
                                    tr.mm(bank(7), wqp[:, kc, hp * 128:(hp + 1) * 128], CQ[:, kc, tb * 512:(tb + 1) * 512], kc == 0, kc == 2,
                                          r=["wqp"] + cqk(tb), w=pk(7))
                                tr.tt("dve", TA[0][:], bank(7), cos2[:, tb * 512:(tb + 1) * 512], ALU.mult, r=pk(7) + ["cos2"], w=[("TA", 0)])
                            tasks.append(tpa)

                            def tpb(tb=tb):
                                for kc in range(3):
                                    tr.mm(bank(7), wqps[:, kc, hp * 128:(hp + 1) * 128], CQ[:, kc, tb * 512:(tb + 1) * 512], kc == 0, kc == 2,
                                          r=["wqps"] + cqk(tb), w=pk(7))
                                tr.tt("dve", TB[0][:], bank(7), sin2[:, tb * 512:(tb + 1) * 512], ALU.mult, r=pk(7) + ["sin2"], w=[("TB", 0)])
                                tr.tt("pool", QP[psl][:, tb * 512:(tb + 1) * 512], TA[0][:], TB[0][:], ALU.add,
                                      r=[("TA", 0), ("TB", 0)], w=[("QP", psl, tb)])
                            tasks.append(tpb)
                    for g in range(4):
                        def tv(g=g):
                            for ii in range(4):
                                i = g * 4 + ii
                                for kc in range(2):
                                    tr.mm(ps[:, 7 * 512 + ii * 128: 7 * 512 + (ii + 1) * 128], CKV[:, kc, i * 128:(i + 1) * 128],
                                          wv[:, kc, h * 128:(h + 1) * 128], kc == 0, kc == 1, r=["wv"] + ckvk(g), w=pk(7))
                            tr.copy("act", VH[sl][:, g * 4:(g + 1) * 4, :], bank(7).rearrange("p (a n) -> p a n", n=128), r=pk(7), w=[("VH", sl, g)])
                        tasks.append(tv)
                    return tasks

                steps = [(h, qb, kt) for h in range(8) for qb in range(4) for kt in range(NT)]

                def emit_S(i):
                    h, qb, kt = steps[i]
                    sl = h % 2
                    psl = (h // 2) % 2
                    rows = slice(0, 64) if h % 2 == 0 else slice(64, 128)
                    sbk = i % 3
                    tr.mm(bank(sbk), KN[sl][:, kt * 128:(kt + 1) * 128], QN[sl][:, qb * 512:(qb + 1) * 512], True, False,
                          r=[("KN", sl, kt // 4), ("QN", sl, qb)], w=pk(sbk))
                    tr.mm(bank(sbk), KPE[:, h % 2, kt * 128:(kt + 1) * 128], QP[psl][:, qb * 512:(qb + 1) * 512], False, True,
                          r=[("KPE", kt // 4), ("QP", psl, qb)], w=pk(sbk))
                    pt = i % NPT
                    tr.act(PT[pt][:], bank(sbk), AF.Exp, scale=SCALE, r=pk(sbk), w=[("PT", pt)])

                deferred = []

                def emit_PV(i):
                    h, qb, kt = steps[i]
                    sl = h % 2
                    par = (h * 4 + qb) % 2
                    ob, lb = 3 + par, 5 + par
                    pt = i % NPT
                    tr.mm(bank(ob), VH[sl][:, kt, :], PT[pt][:], kt == 0, kt == NT - 1, r=[("VH", sl, kt // 4), ("PT", pt)], w=pk(ob))
                    if kt % 2 == 1:
                        tr.mm(bank(lb), ones[:], PT[pt][:], kt == 1, False, r=["ones", ("PT", pt)], w=pk(lb))
                    else:
                        eng, acc, akey = ("dve", ACD[par], ("ACD", par)) if kt % 4 == 0 else ("pool", ACP[par], ("ACP", par))
                        if kt < 4:
                            tr.copy(eng, acc[:], PT[pt][:], r=[("PT", pt)], w=[akey])
                        else:
                            tr.tt(eng, acc[:], acc[:], PT[pt][:], ALU.add, r=[("PT", pt), akey], w=[akey])
                    if kt == NT - 1:
                        def epi(h=h, qb=qb, par=par, ob=ob, lb=lb):
                            tr.mm(bank(lb), onesf[:], ACD[par][:], False, False, r=["onesf", ("ACD", par)], w=pk(lb))
                            tr.mm(bank(lb), onesf[:], ACP[par][:], False, True, r=["onesf", ("ACP", par)], w=pk(lb))
                            tr.recip(RL[:], bank(lb), r=pk(lb), w=["RL"])
                            tr.tt("dve", TA[1][:], bank(ob), RL[:], ALU.mult, r=pk(ob) + ["RL"], w=[("TA", 1)])
                            tr.tt("pool", GATE[:, h, qb * 512:(qb + 1) * 512], GATE[:, h, qb * 512:(qb + 1) * 512], TA[1][:], ALU.mult,
                                  r=[("GATE", h, qb), ("TA", 1)], w=[("GATE", h, qb)])
                        deferred.append((i + 3, epi))

                for t in proj_tasks(0):
                    t()
                LOOK = 2
                pending = []
                for i in range(min(LOOK, len(steps))):
                    emit_S(i)
                for i in range(len(steps)):
                    h, qb, kt = steps[i]
                    if qb == 0 and kt == 0 and h < 7:
                        pending = proj_tasks(h + 1)
                    if i + LOOK < len(steps):
                        emit_S(i + LOOK)
                    emit_PV(i)
                    while deferred and deferred[0][0] <= i:
                        deferred.pop(0)[1]()
                    if pending and (i % 2 == 1):
                        pending.pop(0)()
                assert not pending
                while deferred:
                    deferred.pop(0)[1]()

                tr.flush()
                ses.close()
                ses = ExitStack()
                XT = [ssb("XTd%d" % i, [128, D], F32) for i in range(3)]
                HNR = [ssb("HNRd%d" % i, [128, D], BF16) for i in range(3)]
                junk = ssb("junkd", [128, D], BF16)

                def d_front(i):
                    sl = i % 3
                    tr.dma("sp", XT[sl][:], A["x"][b, i * 128:(i + 1) * 128, :], w=[("XT", sl)])
                    for hf in range(2):
                        bk = 2 * sl + hf
                        for kc in range(8):
                            tr.mm(bank(bk), GATE[:, kc, i * 128:(i + 1) * 128], WO[:, kc, hf * 512:(hf + 1) * 512], kc == 0, kc == 7,
                                  r=["WO"] + [("GATE", kc, i // 4)], w=pk(bk))
                    tr.tt("dve", XT[sl][:], bank(2 * sl, 2), XT[sl][:], ALU.add, r=pk(2 * sl, 2) + [("XT", sl)], w=[("XT", sl)])
                    tr.dma("act", h1s[b, i * 128:(i + 1) * 128, :], XT[sl][:], r=[("XT", sl)], w=[("h1s", i)])
                    norm_part(XT[sl], ("XT", sl), sl)

                for i in range(NT + 2):
                    if i < NT:
                        d_front(i)
                    if i >= 2:
                        tr_part(i - 2, (i - 2) % 3, 6)
                tr.flush()
                ses.close()

            with ExitStack() as les:
                lsb = lambda name, shape, dt: les.enter_context(nc.sbuf_tensor(uq(name), shape, dt))
                UT = lsb("UT", [128, 16, 1024], BF16)
                ZG = lsb("ZG", [128, 8, T], BF16)
                WO = lsb("WO1", [128, 8, 1024], BF16)
                hnt_all = [("HNT", i) for i in range(NT)]
                bes = ExitStack()
                bsb = lambda name, shape, dt: bes.enter_context(nc.sbuf_tensor(uq(name), shape, dt))
                W0 = [bsb("W1_%d" % i, [128, 8, 512], BF16) for i in range(4)]
                T1 = [bsb("T1_%d" % i, [128, T], F32) for i in range(2)]
                XC = [bsb("XC%d" % i, [128, T], BF16) for i in range(2)]
                SG = bsb("SG", [128, T], BF16)
                XTM4 = bsb("XTM4", [128, 16, 4, 128], BF16)

                w1_slot = {}
                w1_n = [0]

                def w1_load(g):
                    if g not in w1_slot:
                        sl = w1_n[0] % 4
                        w1_n[0] += 1
                        tr.dma("sp", W0[sl][:], w1s[:, :, g * 512:(g + 1) * 512], w=[("W0", sl)])
                        w1_slot[g] = sl
                    return w1_slot[g]

                def inproj1(fc, n, pbase):
                    sl = w1_load(fc // 4)
                    off = (fc % 4) * 128
                    for tb in range(4):
                        bk = pbase + tb
                        for kc in range(8):
                            tr.mm(bank(bk), W0[sl][:, kc, off:off + 128], HNT[:, kc, tb * 512:(tb + 1) * 512], kc == 0, kc == 7,
                                  r=[("W0", sl)], w=pk(bk))

                def shortconv(fc, n, pbase):
                    s = n % 2
                    u = ps[:, pbase * 512:pbase * 512 + T]
                    kk = pk(pbase, 4)
                    tr.act(T1[s][:], u, AF.Identity, scale=convw[:, 1, fc:fc + 1], bias=convb[:, fc:fc + 1],
                           r=kk + [("convw", 1), "convb"], w=[("T1", s)])
                    tr.stt("dve", T1[s][:, 1:T], u[:, 0:T - 1], convw[:, 0, fc:fc + 1], T1[s][:, 1:T], ALU.mult, ALU.add,
                           r=kk + [("T1", s), ("convw", 0)], w=[("T1", s)])
                    tr.stt("dve", XC[s][:, 0:T - 1], u[:, 1:T], convw[:, 2, fc:fc + 1], T1[s][:, 0:T - 1], ALU.mult, ALU.add,
                           r=kk + [("T1", s), ("convw", 2)], w=[("XC", s)])
                    tr.copy("pool", XC[s][:, T - 1:T], T1[s][:, T - 1:T], r=[("T1", s)], w=[("XC", s)])

                chunks = [("x1", c, c) for c in range(8)] + [("v", c, 16 + c) for c in range(8)]

                def transposes(n):
                    which, c, fc = chunks[n]
                    s = n % 2
                    pb = 4 * s
                    for i in range(NT):
                        tr.mm(ps[:, pb * 512 + i * 128: pb * 512 + (i + 1) * 128], XC[s][:, i * 128:(i + 1) * 128], ident[:], True, True,
                              r=[("XC", s), "ident"], w=pk(pb, 4))
                    src = bank(pb, 4).rearrange("p (a n) -> p a n", n=128)
                    if which == "v":
                        tr.copy("act", UT[:, :, c * 128:(c + 1) * 128], src, r=pk(pb, 4), w=[("UT", c // 4)])
                    else:
                        tr.copy("act", XTM4[:, :, c % 4, :], src, r=pk(pb, 4), w=["XTM4"])
                        if c % 4 == 3:
                            cbx = c // 4
                            tr.dma("act", x1t[b, :, cbx * 512:(cbx + 1) * 512].rearrange("(a p) (c n) -> p a c n", p=128, n=128),
                                   XTM4[:], r=["XTM4"], w=["x1t"])

                for n in range(len(chunks)):
                    which, c, fc = chunks[n]
                    pbase = 4 * (n % 2)
                    inproj1(fc, n, pbase)
                    if n >= 1:
                        transposes(n - 1)
                    shortconv(fc, n, pbase)
                transposes(len(chunks) - 1)
                n = len(chunks)
                for c in range(8):
                    inproj1(24 + c, n, 0)
                    tr.act(SG[:], ps[:, 0:T], AF.Silu, r=pk(0, 4), w=["SG"])
                    n += 1
                    inproj1(8 + c, n, 4)
                    shortconv(8 + c, c, 4)
                    s = c % 2
                    tr.tt("pool", XC[s][:], XC[s][:], SG[:], ALU.mult, r=[("XC", s), "SG"], w=[("XC", s)])
                    tr.dma("act", xgs[b, c * 128:(c + 1) * 128, :], XC[s][:], r=[("XC", s)], w=[("xgs", c)])
                    n += 1

                tr.flush()
                bes.close()
                bes = ExitStack()
                Z = bsb("Z", [128, 16, 2, 512], BF16)
                FC = [bsb("FC%d" % i, [128, 16, 128], BF16) for i in range(2)]
                FS = [bsb("FS%d" % i, [128, 16, 128], BF16) for i in range(2)]
                KF = [bsb("KF%d" % i, [128, 2, 512], F32) for i in range(2)]
                M = [bsb("M%d" % i, [128, 512], F32) for i in range(4)]
                X1 = [bsb("X1_%d" % i, [128, 512], BF16) for i in range(2)]
                Y2 = [bsb("Y2_%d" % i, [128, 512], BF16) for i in range(2)]
                XG = bsb("XG", [128, 4, T], BF16)
                nf = 0
                for cb in range(2):
                    utk = [("UT", cb)]
                    for order in range(2):
                        for j in range(16):
                            s = nf % 2
                            nf += 1
                            tr.dma("sp", FC[s][:], A["c_c2f"][j], w=[("FC", s)])
                            tr.dma("sp", FS[s][:], A["c_s2f"][j], w=[("FS", s)])
                            tr.dma("sp", KF[s][:], kfs[order, j, :, :, cb * 512:(cb + 1) * 512], w=[("KF", s)])
                            ba, bb = 2 * s, 2 * s + 1
                            for sc in range(16):
                                tr.mm(bank(ba), FC[s][:, sc, :], UT[:, sc, cb * 512:(cb + 1) * 512], sc == 0, sc == 15,
                                      r=[("FC", s)] + utk, w=pk(ba))
                            for sc in range(16):
                                tr.mm(bank(bb), FS[s][:, sc, :], UT[:, sc, cb * 512:(cb + 1) * 512], sc == 0, sc == 15,
                                      r=[("FS", s)] + utk, w=pk(bb))
                            kr, ki = KF[s][:, 0, :], KF[s][:, 1, :]
                            tr.tt("dve", M[0][:], bank(ba), kr, ALU.mult, r=pk(ba) + [("KF", s)], w=[("M", 0)])
                            tr.tt("dve", M[1][:], bank(bb), ki, ALU.mult, r=pk(bb) + [("KF", s)], w=[("M", 1)])
                            tr.tt("dve", M[2][:], bank(bb), kr, ALU.mult, r=pk(bb) + [("KF", s)], w=[("M", 2)])
                            tr.tt("dve", M[3][:], bank(ba), ki, ALU.mult, r=pk(ba) + [("KF", s)], w=[("M", 3)])
                            tr.tt("pool", Z[:, j, 0, :], M[0][:], M[1][:], ALU.add, r=[("M", 0), ("M", 1)], w=[("Z", j)])
                            tr.tt("pool", Z[:, j, 1, :], M[2][:], M[3][:], ALU.subtract, r=[("M", 2), ("M", 3)], w=[("Z", j)])
                        if order == 0:
                            for cc in range(4):
                                tr.dma("sp", XG[:, cc, :], xgs[b, (cb * 4 + cc) * 128:(cb * 4 + cc + 1) * 128, :],
                                       r=[("xgs", cb * 4 + cc)], w=[("XG", cc)])
                            if cb == 0:
                                tr.dma("sp", WO[:], wo1s, w=["WO"])
                        zk = [("Z", j) for j in range(16)]
                        for tc in range(16):
                            s = nf % 2
                            nf += 1
                            tr.dma("sp", FC[s][:], A["c_c2f"][tc], w=[("FC", s)])
                            tr.dma("sp", FS[s][:], A["c_s2f"][tc], w=[("FS", s)])
                            bo = 4 + s
                            for fcn in range(16):
                                tr.mm(bank(bo), FC[s][:, fcn, :], Z[:, fcn, 0, :], fcn == 0, False, r=[("FC", s)] + zk, w=pk(bo), inc=False)
                                tr.mm(bank(bo), FS[s][:, fcn, :], Z[:, fcn, 1, :], False, fcn == 15, r=[("FS", s)] + zk, w=pk(bo), inc=(fcn == 15))
                            if order == 0:
                                tr.dma("sp", X1[s][:], x1t[b, tc * 128:(tc + 1) * 128, cb * 512:(cb + 1) * 512], r=["x1t"], w=[("X1", s)])
                                tr.tt("dve", UT[:, tc, cb * 512:(cb + 1) * 512], bank(bo), X1[s][:], ALU.mult,
                                      r=pk(bo) + [("X1", s)], w=utk)
                            else:
                                tr.copy("act", Y2[s][:], bank(bo), r=pk(bo), w=[("Y2", s)])
                                bt_ = 6 + s
                                for cc in range(4):
                                    tr.mm(ps[:, bt_ * 512 + cc * 128: bt_ * 512 + (cc + 1) * 128], Y2[s][:, cc * 128:(cc + 1) * 128], ident[:],
                                          True, True, r=[("Y2", s), "ident"], w=pk(bt_), inc=(cc == 3))
                                tr.tt("dve", ZG[:, cb * 4:(cb + 1) * 4, tc * 128:(tc + 1) * 128],
                                      bank(bt_).rearrange("p (a n) -> p a n", n=128), XG[:, :, tc * 128:(tc + 1) * 128], ALU.mult,
                                      r=pk(bt_) + [("XG", cc) for cc in range(4)], w=[("ZG", cb, tc)])

                tr.flush()
                bes.close()
                bes = ExitStack()
                H1T = [bsb("H1L%d" % i, [128, D], F32) for i in range(4)]
                OT = [bsb("OT%d" % i, [128, D], F32) for i in range(4)]
                junk = bsb("junk1", [128, D], BF16)
                st = bsb("st1", [128, 16], F32)
                for i in range(NT):
                    s = i % 4
                    c0 = 3 * s
                    sk = ("st", s)
                    tr.dma("sp", H1T[s][:], h1s[b, i * 128:(i + 1) * 128, :], r=[("h1s", i)], w=[("H1T", s)])
                    for hf in range(2):
                        bk = 2 * s + hf
                        for kc in range(8):
                            tr.mm(bank(bk), ZG[:, kc, i * 128:(i + 1) * 128], WO[:, kc, hf * 512:(hf + 1) * 512], kc == 0, kc == 7,
                                  r=["WO", ("ZG", kc // 4, i)], w=pk(bk))
                    tr.tt("dve", H1T[s][:], bank(2 * s, 2), H1T[s][:], ALU.add, r=pk(2 * s, 2) + [("H1T", s)], w=[("H1T", s)])
                    tr.act(junk[:], H1T[s][:], AF.Square, accum=st[:, c0:c0 + 1], r=[("H1T", s)], w=[sk])
                    tr.act(st[:, c0 + 1:c0 + 2], st[:, c0:c0 + 1], AF.Sqrt, scale=1.0 / D, bias=epst[:, 0:1], r=[sk, "eps"], w=[sk])
                    tr.recip(st[:, c0 + 2:c0 + 3], st[:, c0 + 1:c0 + 2], r=[sk], w=[sk])
                    tr.stt("dve", OT[s][:], H1T[s][:], st[:, c0 + 2:c0 + 3], gfin[:], ALU.mult, ALU.mult,
                           r=[("H1T", s), sk, "gfin"], w=[("OT", s)])
                    tr.dma("act", out_ap[b, i * 128:(i + 1) * 128, :], OT[s][:], r=[("OT", s)], w=[("out", i)])
                tr.flush()
                bes.close()
    return nc


_NC_CACHE = {}


def _run(inputs, nseq_per_core, ncores, dbg=False):
    key = (nseq_per_core, dbg)
    if key not in _NC_CACHE:
        _NC_CACHE[key] = build_program(nseq_per_core, dbg)
    nc = _NC_CACHE[key]
    cs = _consts()
    x = np.ascontiguousarray(np.asarray(inputs["x"], dtype=np.float32))
    in_maps = []
    for c in range(ncores):
        m = {"x": np.ascontiguousarray(x[c * nseq_per_core:(c + 1) * nseq_per_core])}
        for name, _ in PARAMS:
            m[name] = np.ascontiguousarray(np.asarray(inputs[name], dtype=np.float32))
        m.update(cs)
        in_maps.append(m)
    res = run_bass_kernel_spmd(nc, in_maps, core_ids=list(range(ncores)))
    return res


def kernel(**inputs):
    res = _run(inputs, SEQ_PER_CORE, NCORES)
    return np.concatenate([r["out"] for r in res.results], axis=0)
```

```python
import math
from contextlib import ExitStack

import numpy as np
import ml_dtypes
import concourse.bass as bass
import concourse.mybir as mybir
from concourse.bass_utils import run_bass_kernel_spmd

F32 = mybir.dt.float32
BF16 = mybir.dt.bfloat16
AF = mybir.ActivationFunctionType
ALU = mybir.AluOpType

T = 2048
D = 1024
NT = 16
NCORES = 8
SEQ_PER_CORE = 4
EPS = 1e-6
NFFT = 4096
SCALE = 192 ** -0.5


class TR:
    ENG = ("pe", "act", "dve", "pool", "sp")
    CE = ("pe", "act", "dve", "pool")

    def __init__(self, nc, es, nds=24):
        self.nc = nc
        self.sem = {e: es.enter_context(nc.semaphore("s_" + e)) for e in self.CE}
        self.base = {e: 0 for e in self.CE}
        self.nidx = {e: 0 for e in self.CE}
        self.ops = {e: [] for e in self.ENG}
        self.seen = {e: {} for e in self.ENG}
        self.lastw = {}
        self.readers = {}
        self.nds = nds
        self.dsem = {q: [es.enter_context(nc.semaphore("d_%s%d" % (q, i))) for i in range(nds)] for q in ("sp", "pool", "act")}
        self.dval = {q: [0] * nds for q in self.dsem}
        self.dn = {q: 0 for q in self.dsem}

    def _deps(self, e, r, w):
        evs = []
        for k in r:
            if k in self.lastw:
                evs.append(self.lastw[k])
        for k in w:
            if k in self.lastw:
                evs.append(self.lastw[k])
            evs.extend(self.readers.get(k, {}).values())
        out = []
        seen = self.seen[e]
        for (name, sem, val) in evs:
            if name == "pe" and e == "pe":
                continue
            if seen.get(name, 0) >= val:
                continue
            seen[name] = val
            out.append((name, sem, val))
        return out

    def _commit(self, e, ev, r, w):
        for k in r:
            self.readers.setdefault(k, {})[ev[0]] = ev
        for k in w:
            self.lastw[k] = ev
            self.readers[k] = {}

    def op(self, e, fn, r=(), w=(), inc=True):
        waits = self._deps(e, r, w)
        self.nidx[e] += 1
        ev = (e, None, self.nidx[e])
        self.ops[e].append([waits, fn, ("ce", e, self.nidx[e])])
        self._commit(e, ev, r, w)

    def dma(self, q, out, in_, r=(), w=(), slow=False):
        i = self.dn[q] % self.nds
        self.dn[q] += 1
        sem = self.dsem[q][i]
        waits = self._deps(q, r, w)
        name = "d_%s%d" % (q, i)
        if self.dval[q][i] > 0 and self.seen[q].get(name, 0) < self.dval[q][i]:
            waits.append((name, sem, self.dval[q][i]))
            self.seen[q][name] = self.dval[q][i]
        self.dval[q][i] += 16
        ev = (name, sem, self.dval[q][i])
        if slow:
            fn = lambda eng, o=out, a=in_: eng.dma_start(out=o, in_=a, allow_slow_non_contiguous=True)
        else:
            fn = lambda eng, o=out, a=in_: eng.dma_start(out=o, in_=a)
        if q in self.CE:
            self.nidx[q] += 1
        self.ops[q].append([waits, fn, ("dma", sem, 16)])
        self._commit(q, ev, r, w)

    def mm(self, out, lhsT, rhs, start, stop, r=(), w=(), inc=None):
        self.op("pe", lambda e: e.matmul(out, lhsT, rhs, start=start, stop=stop), r, w)

    def act(self, out, in_, func, r=(), w=(), scale=1.0, bias=None, accum=None):
        def fn(e):
            kw = {}
            if bias is not None:
                kw["bias"] = bias
            if accum is not None:
                kw["accum_out"] = accum
            return e.activation(out=out, in_=in_, func=func, scale=scale, **kw)
        self.op("act", fn, r, w)

    def tt(self, eng, out, in0, in1, op, r=(), w=()):
        self.op(eng, lambda e: e.tensor_tensor(out=out, in0=in0, in1=in1, op=op), r, w)

    def ts(self, eng, out, in0, s1, s2, op0, op1=None, r=(), w=()):
        if op1 is None:
            self.op(eng, lambda e: e.tensor_scalar(out=out, in0=in0, scalar1=s1, scalar2=None, op0=op0), r, w)
        else:
            self.op(eng, lambda e: e.tensor_scalar(out=out, in0=in0, scalar1=s1, scalar2=s2, op0=op0, op1=op1), r, w)

    def stt(self, eng, out, in0, scalar, in1, op0, op1, r=(), w=()):
        self.op(eng, lambda e: e.scalar_tensor_tensor(out=out, in0=in0, scalar=scalar, in1=in1, op0=op0, op1=op1), r, w)

    def copy(self, eng, out, in_, r=(), w=()):
        if eng == "act":
            self.op("act", lambda e: e.activation(out=out, in_=in_, func=AF.Copy), r, w)
        else:
            self.op(eng, lambda e: e.tensor_copy(out=out, in_=in_), r, w)

    def recip(self, out, in_, r=(), w=()):
        self.op("dve", lambda e: e.reciprocal(out=out, in_=in_), r, w)

    def memset(self, eng, ap, val, w=()):
        self.op(eng, lambda e: e.memset(ap, val), (), w)

    def flush(self, final=False):
        nc = self.nc
        last = {}
        for f in self.CE:
            li = 0
            for waits, fn, tag in self.ops[f]:
                if tag[0] == "ce":
                    li = tag[2]
            last[f] = li
        for e in self.ENG:
            waits = []
            for f in self.CE:
                if f != e and last[f] > self.seen[e].get(f, 0):
                    waits.append((f, None, last[f]))
                    self.seen[e][f] = last[f]
            for q in self.dsem:
                for i in range(self.nds):
                    name = "d_%s%d" % (q, i)
                    if self.dval[q][i] > self.seen[e].get(name, 0):
                        waits.append((name, self.dsem[q][i], self.dval[q][i]))
                        self.seen[e][name] = self.dval[q][i]
            self.ops[e].append([waits, None, None])
        needed = {f: set() for f in self.CE}
        for e in self.ENG:
            for waits, fn, tag in self.ops[e]:
                for (name, sem, val) in waits:
                    if name in needed:
                        needed[name].add(val)
        rank = {}
        for f in self.CE:
            srt = sorted(needed[f])
            rank[f] = {idx: self.base[f] + k + 1 for k, idx in enumerate(srt)}
            self.base[f] += len(srt)
        ops = self.ops
        self.ops = {e: [] for e in self.ENG}
        self.lastw = {}
        self.readers = {}
        sems = self.sem

        def replay(eng, lst):
            for waits, fn, tag in lst:
                for (name, sem, val) in waits:
                    if name in rank:
                        eng.wait_ge(sems[name], rank[name][val])
                    else:
                        eng.wait_ge(sem, val)
                if fn is not None:
                    ins = fn(eng)
                    if tag[0] == "dma":
                        ins.then_inc(tag[1], tag[2])
                    elif tag[2] in rank[tag[1]]:
                        ins.then_inc(sems[tag[1]], 1)

        with nc.Block() as block:
            @block.tensor
            def _(e):
                replay(e, ops["pe"])

            @block.scalar
            def _(e):
                replay(e, ops["act"])

            @block.vector
            def _(e):
                replay(e, ops["dve"])

            @block.gpsimd
            def _(e):
                replay(e, ops["pool"])

            @block.sync
            def _(e):
                replay(e, ops["sp"])


_CONSTS = None


def _consts():
    global _CONSTS
    if _CONSTS is not None:
        return _CONSTS
    bf = ml_dtypes.bfloat16
    idx = np.arange(2048, dtype=np.float64) + 0.5
    ang = 2.0 * np.pi * np.outer(idx, idx) / NFFT
    c2 = np.cos(ang)
    s2 = np.sin(ang)

    def tile_lhsT(m):
        return np.ascontiguousarray(m.reshape(16, 128, 16, 128).transpose(2, 1, 0, 3)).astype(bf)

    c2f = tile_lhsT(c2)
    s2f = tile_lhsT(s2)
    d = np.arange(2048, dtype=np.float64)
    angm = 2.0 * np.pi * np.outer(d, idx) / NFFT
    cmt = tile_lhsT(np.cos(angm))
    smt = tile_lhsT(np.sin(angm))
    f32 = np.float32
    t = np.linspace(0.0, 1.0, T, dtype=f32)[:, None]
    min_decay = math.log(1e-2) / 1.5
    max_decay = math.log(1e-2) / 0.3
    deltas = np.abs(np.linspace(min_decay, max_decay, 1024, dtype=f32))
    decay = np.exp(-t * deltas[None, :]).astype(f32)
    w = (f32(2.0 * math.pi) * np.arange(T, dtype=f32) / f32(T)).astype(f32)
    bands = np.linspace(1e-4, 15, 16, dtype=f32)
    fw = (w[:, None] * bands[None, :]).astype(f32)
    z = np.concatenate([t, np.cos(fw), -np.sin(fw)], axis=-1).astype(f32)
    zt = np.ascontiguousarray(z.T)
    inv = (1.0 / (f32(10000.0) ** (np.arange(0, 64, 2, dtype=f32) / f32(64)))).astype(f32)
    angr = (np.arange(T, dtype=f32)[:, None] * inv[None, :]).astype(f32)
    cos = np.cos(angr).T.astype(f32)
    sin = np.sin(angr).T.astype(f32)
    cos2 = np.ascontiguousarray(np.concatenate([cos, cos, cos, cos], axis=0))
    sin2 = np.ascontiguousarray(np.concatenate([-sin, sin, -sin, sin], axis=0))
    ident = np.eye(128, dtype=np.float32).astype(bf)
    _CONSTS = dict(c_ident=ident, c_c2f=c2f, c_s2f=s2f, c_cmt=cmt, c_smt=smt, c_decay=decay,
                   c_zt=zt, c_cos2=cos2, c_sin2=sin2)
    return _CONSTS


PARAMS = [
    ("l0_norm", [1024]), ("l0_w_in", [1024, 1728]), ("l0_q_norm", [384]), ("l0_w_uq", [384, 1536]),
    ("l0_kv_norm", [256]), ("l0_w_ukv", [256, 2048]), ("l0_w_out", [1024, 1024]),
    ("l1_norm", [1024]), ("l1_w_in", [1024, 4096]), ("l1_conv_w", [3, 3072]), ("l1_conv_b", [3072]),
    ("l1_filt_w1", [33, 64]), ("l1_filt_b1", [64]), ("l1_filt_w2", [64, 64]), ("l1_filt_b2", [64]),
    ("l1_filt_w3", [64, 64]), ("l1_filt_b3", [64]), ("l1_filt_w4", [64, 4096]), ("l1_filt_freq", [64]),
    ("l1_filt_bias", [2, 1024]), ("l1_w_out", [1024, 1024]), ("final_norm", [1024]),
]


def build_program(nseq=SEQ_PER_CORE, dbg=False):
    nc = bass.Bass("TRN2", target_bir_lowering=False)
    H = {}
    H["x"] = nc.dram_tensor("x", [nseq, T, D], F32, kind="ExternalInput")
    for name, shape in PARAMS:
        H[name] = nc.dram_tensor(name, shape, F32, kind="ExternalInput")
    cs = _consts()
    for name, arr in cs.items():
        H[name] = nc.dram_tensor(name, list(arr.shape), BF16 if arr.dtype != np.float32 else F32, kind="ExternalInput")
    out_h = nc.dram_tensor("out", [nseq, T, D], F32, kind="ExternalOutput")
    A = {k: v.ap() for k, v in H.items()}
    out_ap = out_h.ap()

    def scratch(name, shape, dt):
        return nc.dram_tensor(name, shape, dt, kind="ExternalOutput" if (dbg and name in ("h1s",)) else "Internal").ap()

    w0s = scratch("w0s", [128, 8, 1920], BF16)
    wo0s = scratch("wo0s", [128, 8, 1024], BF16)
    w1s = scratch("w1s", [128, 8, 4096], BF16)
    wo1s = scratch("wo1s", [128, 8, 1024], BF16)
    kfs = scratch("kfs", [2, 16, 128, 2, 1024], F32)
    h1s = scratch("h1s", [nseq, T, D], F32)
    x1t = scratch("x1t", [nseq, T, D], BF16)
    xgs = scratch("xgs", [nseq, D, T], BF16)
    wqn_s = scratch("wqn_s", [128, 3, 1024], BF16)
    wqp_s = scratch("wqp_s", [128, 3, 512], BF16)
    wqps_s = scratch("wqps_s", [128, 3, 512], BF16)
    wkn_s = scratch("wkn_s", [128, 2, 1024], BF16)
    wv_s = scratch("wv_s", [128, 2, 1024], BF16)

    _uq = [0]

    def uq(name):
        _uq[0] += 1
        return "%s_%d" % (name, _uq[0])

    with ExitStack() as es:
        tr = TR(nc, es)
        sb = lambda name, shape, dt: es.enter_context(nc.sbuf_tensor(uq(name), shape, dt))
        ps = es.enter_context(nc.psum_tensor("ps", [128, 4096], F32))

        def bank(b, n=1):
            return ps[:, b * 512:(b + n) * 512]

        def pk(b, n=1):
            return [("ps", b + i) for i in range(n)]

        ident = sb("ident", [128, 128], BF16)
        ones = sb("ones", [128, 128], BF16)
        epst = sb("epst", [128, 1], F32)
        onesf = sb("onesf", [128, 128], F32)
        gfin = sb("gfin", [128, D], F32)
        convw = sb("convw", [128, 3, 24], F32)
        convb = sb("convb", [128, 24], F32)
        HNT = None

        with ExitStack() as pes:
            psb = lambda name, shape, dt: pes.enter_context(nc.sbuf_tensor(uq(name), shape, dt))
            tr.dma("sp", ident[:], A["c_ident"], w=["ident"])
            wqn = psb("wqn", [128, 3, 1024], BF16)
            wqp = psb("wqp", [128, 3, 512], BF16)
            wqps = psb("wqps", [128, 3, 512], BF16)
            wkn = psb("wkn", [128, 2, 1024], BF16)
            wv = psb("wv", [128, 2, 1024], BF16)
            tr.memset("pool", ones[:], 1.0, w=["ones"])
            tr.memset("pool", epst[:], EPS, w=["eps"])
            tr.memset("pool", onesf[:], 1.0, w=["onesf"])
            tr.dma("sp", gfin[:], bass.AP(H["final_norm"], 0, [[0, 128], [1, D]]), w=["gfin"])
            for j in range(3):
                tr.dma("sp", convw[:, j, :], A["l1_conv_w"][j, :].rearrange("(c p) -> p c", p=128), w=[("convw", j)], slow=True)
            tr.dma("sp", convb[:], A["l1_conv_b"].rearrange("(c p) -> p c", p=128), w=["convb"], slow=True)
            g0 = psb("g0", [128, 8], F32)
            g1 = psb("g1", [128, 8], F32)
            gq = psb("gq", [128, 3], F32)
            gkv = psb("gkv", [128, 2], F32)
            tr.dma("sp", g0[:], A["l0_norm"].rearrange("(c p) -> p c", p=128), w=["g0"], slow=True)
            tr.dma("sp", g1[:], A["l1_norm"].rearrange("(c p) -> p c", p=128), w=["g1"], slow=True)
            tr.dma("sp", gq[:], A["l0_q_norm"].rearrange("(c p) -> p c", p=128), w=["gq"], slow=True)
            tr.dma("sp", gkv[:], A["l0_kv_norm"].rearrange("(c p) -> p c", p=128), w=["gkv"], slow=True)

            stg = [psb("stg%d" % i, [128, 2048], F32) for i in range(2)]
            wbt = [psb("wbt%d" % i, [128, 2048], BF16) for i in range(2)]
            cvt_n = [0]

            p1_tasks = []

            def convert(src_ap, ncols, gain, gkey, consume):
                p1_tasks.append(lambda: convert_now(src_ap, ncols, gain, gkey, consume))

            def convert_now(src_ap, ncols, gain, gkey, consume):
                i = cvt_n[0] % 2
                cvt_n[0] += 1
                tr.dma("sp", stg[i][:, 0:ncols], src_ap, w=[("stg", i)])
                if gain is None:
                    tr.copy("act", wbt[i][:, 0:ncols], stg[i][:, 0:ncols], r=[("stg", i)], w=[("wbt", i)])
                else:
                    tr.act(wbt[i][:, 0:ncols], stg[i][:, 0:ncols], AF.Identity, scale=gain, r=[("stg", i), gkey], w=[("wbt", i)])
                consume(wbt[i], ("wbt", i))

            def w0_task(kc):
                i = cvt_n[0] % 2
                cvt_n[0] += 1
                tr.dma("sp", stg[i][:, 0:1728], A["l0_w_in"][kc * 128:(kc + 1) * 128, :], w=[("stg", i)])
                gsc = g0[:, kc:kc + 1]
                pieces = [(0, 640, 0), (640, 64, 640), (704, 64, 640), (768, 32, 672), (800, 32, 640), (832, 32, 672), (864, 32, 640),
                          (896, 1024, 704)]
                for pi, (dst, n_, src) in enumerate(pieces):
                    tr.act(wbt[i][:, dst:dst + n_], stg[i][:, src:src + n_], AF.Identity, scale=gsc,
                           r=[("stg", i), "g0"], w=[("wbt", i)])
                tr.dma("act", w0s[:, kc, :], wbt[i][:, 0:1920], r=[("wbt", i)], w=["w0s"])
            for kc in range(8):
                p1_tasks.append(lambda kc=kc: w0_task(kc))
            for kc in range(8):
                convert(A["l0_w_out"][kc * 128:(kc + 1) * 128, :], 1024, None, None,
                        lambda wb, key, kc=kc: tr.dma("act", wo0s[:, kc, :], wb[:, 0:1024], r=[key], w=["wo0s"]))
                convert(A["l1_w_out"][kc * 128:(kc + 1) * 128, :], 1024, None, None,
                        lambda wb, key, kc=kc: tr.dma("act", wo1s[:, kc, :], wb[:, 0:1024], r=[key], w=["wo1s"]))
                for hf in range(2):
                    convert(A["l1_w_in"][kc * 128:(kc + 1) * 128, hf * 2048:(hf + 1) * 2048], 2048, g1[:, kc:kc + 1], "g1",
                            lambda wb, key, kc=kc, hf=hf: tr.dma(
                                "act", w1s[:, kc, hf * 2048:(hf + 1) * 2048], wb[:, 0:2048], r=[key], w=["w1s"]))
            for kc in range(3):
                def consq(wb, key, kc=kc):
                    v = wb[:, 0:1536].rearrange("p (h n) -> p h n", n=192)
                    tr.copy("dve", wqn[:, kc, :].rearrange("p (h n) -> p h n", n=128), v[:, :, 0:128], r=[key], w=["wqn"])
                    tr.copy("dve", wqp[:, kc, :].rearrange("p (h n) -> p h n", n=64), v[:, :, 128:192], r=[key], w=["wqp"])
                    o = wqps[:, kc, :].rearrange("p (h n) -> p h n", n=64)
                    tr.copy("dve", o[:, :, 0:32], v[:, :, 160:192], r=[key], w=["wqps"])
                    tr.copy("dve", o[:, :, 32:64], v[:, :, 128:160], r=[key], w=["wqps"])
                convert(A["l0_w_uq"][kc * 128:(kc + 1) * 128, :], 1536, gq[:, kc:kc + 1], "gq", consq)
            for kc in range(2):
                def conskv(wb, key, kc=kc):
                    v = wb[:, 0:2048].rearrange("p (h n) -> p h n", n=256)
                    tr.copy("dve", wkn[:, kc, :].rearrange("p (h n) -> p h n", n=128), v[:, :, 0:128], r=[key], w=["wkn"])
                    tr.copy("dve", wv[:, kc, :].rearrange("p (h n) -> p h n", n=128), v[:, :, 128:256], r=[key], w=["wv"])
                convert(A["l0_w_ukv"][kc * 128:(kc + 1) * 128, :], 2048, gkv[:, kc:kc + 1], "gkv", conskv)

            def pop_p1():
                if p1_tasks:
                    p1_tasks.pop(0)()

            pes0 = pes
            psb0 = lambda name, shape, dt: pes0.enter_context(nc.sbuf_tensor(uq(name), shape, dt))
            a3b = psb0("a3b", [64, T], BF16)
            w4b = psb0("w4b", [64, 4096], BF16)
            fbias = psb0("fbias", [1, 2, 1024], F32)
            with ExitStack() as pes:
              psb = lambda name, shape, dt: pes.enter_context(nc.sbuf_tensor(uq(name), shape, dt))
              stgw = [psb("stgw%d" % i, [64, 2048], F32) for i in range(2)]
              fw1 = psb("fw1", [33, 64], F32)
              fw2 = psb("fw2", [64, 64], F32)
              fw3 = psb("fw3", [64, 64], F32)
              fb = psb("fb", [64, 3], F32)
              ffr = psb("ffr", [64, 1], F32)
              ffb = psb("ffb", [64, 3], F32)
              zt = psb("zt", [33, T], F32)
              aa = [psb("aa%d" % i, [64, T], F32) for i in range(2)]
              m1 = psb("m1", [64, T], F32)
              m2 = psb("m2", [64, T], F32)
              tr.dma("sp", fw1[:], A["l1_filt_w1"], w=["fw1"])
              tr.dma("sp", fw2[:], A["l1_filt_w2"], w=["fw2"])
              tr.dma("sp", fw3[:], A["l1_filt_w3"], w=["fw3"])
              for i, nm in enumerate(("l1_filt_b1", "l1_filt_b2", "l1_filt_b3")):
                  tr.dma("sp", fb[:, i:i + 1], A[nm].rearrange("(p o) -> p o", o=1), w=["fb"], slow=True)
              tr.dma("sp", ffr[:], A["l1_filt_freq"].rearrange("(p o) -> p o", o=1), w=["ffr"], slow=True)
              tr.dma("sp", zt[:], A["c_zt"], w=["zt"])
              tr.dma("sp", fbias[:], A["l1_filt_bias"].rearrange("(o a) c -> o a c", o=1), w=["fbias"])
              for hf in range(2):
                  tr.dma("sp", stgw[hf][0:64, :], A["l1_filt_w4"][:, hf * 2048:(hf + 1) * 2048], w=[("stgw", hf)])
                  tr.copy("dve", w4b[:, hf * 2048:(hf + 1) * 2048], stgw[hf][0:64, :], r=[("stgw", hf)], w=["w4b"])
              tr.ts("dve", ffb[:], fb[:], ffr[:, 0:1], None, ALU.mult, r=["fb", "ffr"], w=["ffb"])
              PI = float(np.pi)
              for layer in range(3):
                  lw = (fw1, fw2, fw3)[layer]
                  src = zt if layer == 0 else aa[(layer - 1) % 2]
                  srck = "zt" if layer == 0 else ("aa", (layer - 1) % 2)
                  kdim = 33 if layer == 0 else 64
                  for tb in range(4):
                      tr.mm(ps[0:64, tb * 512:(tb + 1) * 512], lw[0:kdim, :], src[0:kdim, tb * 512:(tb + 1) * 512], True, True,
                            r=[srck, "fw%d" % (layer + 1)], w=pk(tb))
                  dst = aa[layer % 2]
                  dk = ("aa", layer % 2)
                  tr.ts("dve", dst[:], ps[0:64, 0:T], ffr[:, 0:1], ffb[:, layer:layer + 1], ALU.mult, ALU.add,
                        r=pk(0, 4) + ["ffr", "ffb"], w=[dk])
                  tr.ts("dve", m1[:], dst[:], PI, -2.0 * PI, ALU.is_gt, ALU.mult, r=[dk], w=["m1"])
                  tr.ts("dve", m2[:], dst[:], -PI, 2.0 * PI, ALU.is_lt, ALU.mult, r=[dk], w=["m2"])
                  tr.tt("dve", dst[:], dst[:], m1[:], ALU.add, r=[dk, "m1"], w=[dk])
                  tr.tt("dve", dst[:], dst[:], m2[:], ALU.add, r=[dk, "m2"], w=[dk])
                  if layer < 2:
                      tr.act(dst[:], dst[:], AF.Sin, r=[dk], w=[dk])
                  else:
                      tr.act(a3b[:], dst[:], AF.Sin, r=[dk], w=["a3b"])

              tr.flush()
            with ExitStack() as pes:
              psb = lambda name, shape, dt: pes.enter_context(nc.sbuf_tensor(uq(name), shape, dt))
              ksum = psb("ksum", [128, 16, 1024], BF16)
              kdif = psb("kdif", [128, 16, 1024], BF16)
              dct = [psb("dct%d" % i, [128, 1024], F32) for i in range(2)]
              kft = [psb("kft%d" % i, [128, 1024], F32) for i in range(2)]
              kbt = [psb("kbt%d" % i, [128, 1024], F32) for i in range(2)]
              cmT = [psb("cmT%d" % i, [128, 16, 128], BF16) for i in range(2)]
              smT = [psb("smT%d" % i, [128, 16, 128], BF16) for i in range(2)]
              kout = [psb("kout%d" % i, [128, 2, 512], F32) for i in range(2)]
              n_it = 0
              for order in range(2):
                  for tt_ in range(16):
                      i = n_it % 2
                      n_it += 1
                      pop_p1()
                      tr.dma("sp", dct[i][:], A["c_decay"][tt_ * 128:(tt_ + 1) * 128, :], w=[("dct", i)])
                      for part in range(4):
                          col = order * 2048 + part * 512
                          tr.mm(bank(part), a3b[:, tt_ * 128:(tt_ + 1) * 128], w4b[:, col:col + 512], True, True,
                                r=["a3b", "w4b"], w=pk(part))
                      tr.tt("dve", kft[i][:], ps[:, 0:1024], dct[i][:], ALU.mult, r=pk(0, 2) + [("dct", i)], w=[("kft", i)])
                      tr.tt("dve", kbt[i][:], ps[:, 1024:2048], dct[i][:], ALU.mult, r=pk(2, 2) + [("dct", i)], w=[("kbt", i)])
                      tr.tt("pool", ksum[:, tt_, :], kft[i][:], kbt[i][:], ALU.add, r=[("kft", i), ("kbt", i)], w=["ksum"])
                      tr.tt("pool", kdif[:, tt_, :], kft[i][:], kbt[i][:], ALU.subtract, r=[("kft", i), ("kbt", i)], w=["kdif"])
                      if tt_ == 0:
                          tr.tt("pool", ksum[0:1, 0, :], kft[i][0:1, :], fbias[0:1, order, :], ALU.add,
                                r=[("kft", i), "fbias", "ksum"], w=["ksum"])
                  for j in range(16):
                      i = j % 2
                      pop_p1()
                      tr.dma("sp", cmT[i][:], A["c_cmt"][j], w=[("cmT", i)])
                      tr.dma("sp", smT[i][:], A["c_smt"][j], w=[("smT", i)])
                      for ch in range(2):
                          o = (j * 2 + ch) % 2
                          b0 = 4 + 2 * o
                          for dc in range(16):
                              tr.mm(bank(b0), cmT[i][:, dc, :], ksum[:, dc, ch * 512:(ch + 1) * 512], dc == 0, dc == 15,
                                    r=[("cmT", i), "ksum"], w=pk(b0))
                          for dc in range(16):
                              tr.mm(bank(b0 + 1), smT[i][:, dc, :], kdif[:, dc, ch * 512:(ch + 1) * 512], dc == 0, dc == 15,
                                    r=[("smT", i), "kdif"], w=pk(b0 + 1))
                          tr.act(kout[o][:, 0, :], bank(b0), AF.Identity, scale=2.0 / NFFT, r=pk(b0), w=[("kout", o)])
                          tr.act(kout[o][:, 1, :], bank(b0 + 1), AF.Identity, scale=-2.0 / NFFT, r=pk(b0 + 1), w=[("kout", o)])
                          tr.dma("act", kfs[order, j, :, :, ch * 512:(ch + 1) * 512], kout[o][:], r=[("kout", o)], w=["kfs"])
              while p1_tasks:
                  pop_p1()
              tr.dma("act", wqn_s, wqn[:], r=["wqn"], w=["wqn_s"])
              tr.dma("act", wqp_s, wqp[:], r=["wqp"], w=["wqp_s"])
              tr.dma("act", wqps_s, wqps[:], r=["wqps"], w=["wqps_s"])
              tr.dma("act", wkn_s, wkn[:], r=["wkn"], w=["wkn_s"])
              tr.dma("act", wv_s, wv[:], r=["wv"], w=["wv_s"])
              tr.flush()
        HNT = sb("HNT", [128, 8, T], BF16)
        def norm_part_g(src, srck, slot, HNRb, stb, junkb, tag):
            c0 = 3 * slot
            sk = ("st" + tag, slot)
            tr.act(junkb[:], src[:], AF.Square, accum=stb[:, c0:c0 + 1], r=[srck], w=[sk])
            tr.act(stb[:, c0 + 1:c0 + 2], stb[:, c0:c0 + 1], AF.Sqrt, scale=1.0 / D, bias=epst[:, 0:1], r=[sk, "eps"], w=[sk])
            tr.recip(stb[:, c0 + 2:c0 + 3], stb[:, c0 + 1:c0 + 2], r=[sk], w=[sk])
            tr.ts("dve", HNRb[slot][:], src[:], stb[:, c0 + 2:c0 + 3], None, ALU.mult, r=[srck, sk], w=[("HNR" + tag, slot)])

        def tr_part_g(i, slot, pb, HNRb, tag):
            for c in range(8):
                tr.mm(ps[:, pb * 512 + c * 128: pb * 512 + (c + 1) * 128], HNRb[slot][:, c * 128:(c + 1) * 128], ident[:],
                      True, True, r=[("HNR" + tag, slot), "ident"], w=pk(pb, 2))
            tr.copy("act", HNT[:, :, i * 128:(i + 1) * 128], bank(pb, 2).rearrange("p (c n) -> p c n", n=128),
                    r=pk(pb, 2), w=[("HNT", i)])

        for b in range(nseq):
            with ExitStack() as les:
                lsb = lambda name, shape, dt: les.enter_context(nc.sbuf_tensor(uq(name), shape, dt))
                cos2 = lsb("cos2", [128, T], F32)
                sin2 = lsb("sin2", [128, T], F32)
                wqn = lsb("wqn", [128, 3, 1024], BF16)
                wqp = lsb("wqp", [128, 3, 512], BF16)
                wqps = lsb("wqps", [128, 3, 512], BF16)
                wkn = lsb("wkn", [128, 2, 1024], BF16)
                wv = lsb("wv", [128, 2, 1024], BF16)
                st = lsb("st", [128, 16], F32)
                CQ = lsb("CQ", [128, 3, T], BF16)
                CKV = lsb("CKV", [128, 2, T], BF16)
                KPE = lsb("KPE", [128, 2, T], BF16)
                GATE = lsb("GATE", [128, 8, T], BF16)
                WO = lsb("WO", [128, 8, 1024], BF16)
                ses = ExitStack()
                ssb = lambda name, shape, dt: ses.enter_context(nc.sbuf_tensor(uq(name), shape, dt))
                XT = [ssb("XT%d" % i, [128, D], F32) for i in range(3)]
                HNR = [ssb("HNR%d" % i, [128, D], BF16) for i in range(3)]
                junk = ssb("junk", [128, D], BF16)
                W0 = [ssb("W0_%d" % i, [128, 8, 512], BF16) for i in range(2)]
                SQ = [ssb("SQ%d" % i, [128, 512], BF16) for i in range(2)]
                RQ = ssb("RQ", [128, T], F32)
                TA = [ssb("TA%d" % i, [128, 512], F32) for i in range(2)]
                TB = [ssb("TB%d" % i, [128, 512], F32) for i in range(2)]

                def norm_part(src, srck, slot):
                    norm_part_g(src, srck, slot, HNR, st, junk, "")

                def tr_part(i, slot, pb):
                    tr_part_g(i, slot, pb, HNR, "")

                for i in range(NT + 1 if b == 0 else 0):
                    if i < NT:
                        sl = i % 3
                        tr.dma("sp", XT[sl][:], A["x"][b, i * 128:(i + 1) * 128, :], w=[("XT", sl)])
                        norm_part(XT[sl], ("XT", sl), sl)
                    if i >= 1:
                        tr_part(i - 1, (i - 1) % 3, 2 * ((i - 1) % 3))
                hnt_all = [("HNT", i) for i in range(NT)]
                tr.dma("sp", cos2[:], A["c_cos2"], w=["cos2"])
                tr.dma("sp", sin2[:], A["c_sin2"], w=["sin2"])
                tr.dma("sp", wqn[:], wqn_s, w=["wqn"])
                tr.dma("sp", wqp[:], wqp_s, w=["wqp"])
                tr.dma("sp", wqps[:], wqps_s, w=["wqps"])
                tr.dma("sp", wkn[:], wkn_s, w=["wkn"])
                tr.dma("sp", wv[:], wv_s, w=["wv"])

                w0_loaded = {}

                def inproj(fc, consume, wsrc, ps_base=0):
                    g = fc // 4
                    s = g % 2
                    if g not in w0_loaded:
                        ncol = min(512, 1920 - g * 512)
                        tr.dma("sp", W0[s][:, :, 0:ncol], w0s[:, :, g * 512:g * 512 + ncol], r=[], w=[("W0", s)])
                        w0_loaded[g] = True
                    off = (fc % 4) * 128
                    for tb in range(4):
                        bk = ps_base + tb
                        for kc in range(8):
                            tr.mm(bank(bk), W0[s][:, kc, off:off + 128], HNT[:, kc, tb * 512:(tb + 1) * 512], kc == 0, kc == 7,
                                  r=[("W0", s)] + hnt_all[tb * 4:(tb + 1) * 4], w=pk(bk))
                        consume(tb, bk)

                def rms_latent(buf, bkey, nch, fc0, width):
                    for c in range(nch):
                        def cons(tb, bk, c=c):
                            tr.copy("act", buf[:, c, tb * 512:(tb + 1) * 512], bank(bk), r=pk(bk), w=[(bkey, c, tb)])
                            s = (c * 4 + tb) % 2
                            tr.act(SQ[s][:], bank(bk), AF.Square, r=pk(bk), w=[("SQ", s)])
                            tr.mm(bank(4 + tb), ones[:], SQ[s][:], c == 0, c == nch - 1, r=["ones", ("SQ", s)], w=pk(4 + tb))
                        inproj(fc0 + c, cons, None)
                    for tb in range(4):
                        tr.act(RQ[:, tb * 512:(tb + 1) * 512], bank(4 + tb), AF.Sqrt, scale=1.0 / width, bias=epst[:, 0:1],
                               r=pk(4 + tb) + ["eps"], w=[("RQ", tb)])
                        tr.recip(RQ[:, tb * 512:(tb + 1) * 512], RQ[:, tb * 512:(tb + 1) * 512], r=[("RQ", tb)], w=[("RQ", tb)])
                        for c in range(nch):
                            tr.tt("dve", buf[:, c, tb * 512:(tb + 1) * 512], buf[:, c, tb * 512:(tb + 1) * 512],
                                  RQ[:, tb * 512:(tb + 1) * 512], ALU.mult, r=[(bkey, c, tb), ("RQ", tb)], w=[(bkey, c, tb)])

                rms_latent(CQ, "CQ", 3, 0, 384.0)
                rms_latent(CKV, "CKV", 2, 3, 256.0)

                def rope_evac(dst, dkey, tbl_done):
                    pass

                def consA(tb, bk):
                    tr.tt("dve", RQ[:, tb * 512:(tb + 1) * 512], bank(bk), cos2[:, tb * 512:(tb + 1) * 512], ALU.mult,
                          r=pk(bk) + ["cos2"], w=[("RQ", tb)])
                inproj(5, consA, None)

                def consB(tb, bk):
                    tr.tt("dve", TB[tb % 2][:], bank(bk), sin2[:, tb * 512:(tb + 1) * 512], ALU.mult,
                          r=pk(bk) + ["sin2"], w=[("TB", tb % 2)])
                    if tb == 0:
                        tr.memset("pool", KPE[64:128, 0, :], 0.0, w=[("KPE", t_) for t_ in range(4)])
                        tr.memset("pool", KPE[0:64, 1, :], 0.0, w=[("KPE", t_) for t_ in range(4)])
                    tr.tt("pool", KPE[0:64, 0, tb * 512:(tb + 1) * 512], RQ[0:64, tb * 512:(tb + 1) * 512], TB[tb % 2][0:64, :], ALU.add,
                          r=[("RQ", tb), ("TB", tb % 2)], w=[("KPE", tb)])
                    tr.tt("pool", KPE[64:128, 1, tb * 512:(tb + 1) * 512], RQ[64:128, tb * 512:(tb + 1) * 512], TB[tb % 2][64:128, :], ALU.add,
                          r=[("RQ", tb), ("TB", tb % 2)], w=[("KPE", tb)])
                inproj(6, consB, None)
                for g in range(8):
                    def consG(tb, bk, g=g):
                        tr.act(GATE[:, g, tb * 512:(tb + 1) * 512], bank(bk), AF.Silu, r=pk(bk), w=[("GATE", g, tb)])
                    inproj(7 + g, consG, None)

                tr.flush()
                ses.close()
                ses = ExitStack()
                QN = [ssb("QN%d" % i, [128, T], BF16) for i in range(2)]
                KN = [ssb("KN%d" % i, [128, T], BF16) for i in range(2)]
                QP = [ssb("QP%d" % i, [128, T], BF16) for i in range(2)]
                VH = [ssb("VH%d" % i, [128, 16, 128], BF16) for i in range(2)]
                NPT = 6
                PT = [ssb("PT%d" % i, [128, 512], BF16) for i in range(NPT)]
                RL = ssb("RL", [128, 512], F32)
                tr.dma("sp", WO[:], wo0s, w=["WO"])
                ACD = [ssb("ACD%d" % i, [128, 512], F32) for i in range(2)]
                ACP = [ssb("ACP%d" % i, [128, 512], F32) for i in range(2)]
                TA = [ssb("TAc%d" % i, [128, 512], F32) for i in range(2)]
                TB = [ssb("TBc%d" % i, [128, 512], F32) for i in range(2)]
                cqk = lambda tb: [("CQ", c, tb) for c in range(3)]
                ckvk = lambda tb: [("CKV", c, tb) for c in range(2)]

                def proj_tasks(h):
                    sl = h % 2
                    tasks = []
                    for tb in range(4):
                        def tq(tb=tb):
                            for kc in range(3):
                                tr.mm(bank(7), wqn[:, kc, h * 128:(h + 1) * 128], CQ[:, kc, tb * 512:(tb + 1) * 512], kc == 0, kc == 2,
                                      r=["wqn"] + cqk(tb), w=pk(7))
                            tr.copy("act", QN[sl][:, tb * 512:(tb + 1) * 512], bank(7), r=pk(7), w=[("QN", sl, tb)])
                        tasks.append(tq)

                        def tk(tb=tb):
                            for kc in range(2):
                                tr.mm(bank(7), wkn[:, kc, h * 128:(h + 1) * 128], CKV[:, kc, tb * 512:(tb + 1) * 512], kc == 0, kc == 1,
                                      r=["wkn"] + ckvk(tb), w=pk(7))
                            tr.copy("act", KN[sl][:, tb * 512:(tb + 1) * 512], bank(7), r=pk(7), w=[("KN", sl, tb)])
                        tasks.append(tk)
                        if h % 2 == 0:
                            hp = h // 2
                            psl = hp % 2

                            def tpa(tb=tb):
                                for kc in range(3):
                                    tr.mm(bank(7), wqp[:, kc, hp * 128:(hp + 1) * 128], CQ[:, kc, tb * 512:(tb + 1) * 512], kc == 0, kc == 2,
                                          r=["wqp"] + cqk(tb), w=pk(7))
                                tr.tt("dve", TA[0][:], bank(7), cos2[:, tb * 512:(tb + 1) * 512], ALU.mult, r=pk(7) + ["cos2"], w=[("TA", 0)])
                            tasks.append(tpa)

                            def tpb(tb=tb):
                                for kc in range(3):
                                    tr.mm(bank(7), wqps[:, kc, hp * 128:(hp + 1) * 128], CQ[:, kc, tb * 512:(tb + 1) * 512], kc == 0, kc == 2,
                                          r=["wqps"] + cqk(tb), w=pk(7))
                                tr.tt("dve", TB[0][:], bank(7), sin2[:, tb * 512:(tb + 1) * 512], ALU.mult, r=pk(7) + ["sin2"], w=[("TB", 0)])
                                tr.tt("pool", QP[psl][:, tb * 512:(tb + 1) * 512], TA[0][:], TB[0][:], ALU.add,
                                      r=[("TA", 0), ("TB", 0)], w=[("QP", psl, tb)])
                            tasks.append(tpb)
                    for g in range(4):
                        def tv(g=g):
                            for ii in range(4):
                                i = g * 4 + ii
                                for kc in range(2):
                                    tr.mm(ps[:, 7 * 512 + ii * 128: 7 * 512 + (ii + 1) * 128], CKV[:, kc, i * 128:(i + 1) * 128],
                                          wv[:, kc, h * 128:(h + 1) * 128], kc == 0, kc == 1, r=["wv"] + ckvk(g), w=pk(7))
                            tr.copy("act", VH[sl][:, g * 4:(g + 1) * 4, :], bank(7).rearrange("p (a n) -> p a n", n=128), r=pk(7), w=[("VH", sl, g)])
                        tasks.append(tv)
                    return tasks

                steps = [(h, qb, kt) for h in range(8) for qb in range(4) for kt in range(NT)]

                def emit_S(i):
                    h, qb, kt = steps[i]
                    sl = h % 2
                    psl = (h // 2) % 2
                    rows = slice(0, 64) if h % 2 == 0 else slice(64, 128)
                    sbk = i % 3
                    tr.mm(bank(sbk), KN[sl][:, kt * 128:(kt + 1) * 128], QN[sl][:, qb * 512:(qb + 1) * 512], True, False,
                          r=[("KN", sl, kt // 4), ("QN", sl, qb)], w=pk(sbk))
                    tr.mm(bank(sbk), KPE[:, h % 2, kt * 128:(kt + 1) * 128], QP[psl][:, qb * 512:(qb + 1) * 512], False, True,
                          r=[("KPE", kt // 4), ("QP", psl, qb)], w=pk(sbk))
                    pt = i % NPT
                    tr.act(PT[pt][:], bank(sbk), AF.Exp, scale=SCALE, r=pk(sbk), w=[("PT", pt)])

                deferred = []

                def emit_PV(i):
                    h, qb, kt = steps[i]
                    sl = h % 2
                    par = (h * 4 + qb) % 2
                    ob, lb = 3 + par, 5 + par
                    pt = i % NPT
                    tr.mm(bank(ob), VH[sl][:, kt, :], PT[pt][:], kt == 0, kt == NT - 1, r=[("VH", sl, kt // 4), ("PT", pt)], w=pk(ob))
                    if kt % 2 == 1:
                        tr.mm(bank(lb), ones[:], PT[pt][:], kt == 1, False, r=["ones", ("PT", pt)], w=pk(lb))
                    else:
                        eng, acc, akey = ("dve", ACD[par], ("ACD", par)) if kt % 4 == 0 else ("pool", ACP[par], ("ACP", par))
                        if kt < 4:
                            tr.copy(eng, acc[:], PT[pt][:], r=[("PT", pt)], w=[akey])
                        else:
                            tr.tt(eng, acc[:], acc[:], PT[pt][:], ALU.add, r=[("PT", pt), akey], w=[akey])
                    if kt == NT - 1:
                        def epi(h=h, qb=qb, par=par, ob=ob, lb=lb):
                            tr.mm(bank(lb), onesf[:], ACD[par][:], False, False, r=["onesf", ("ACD", par)], w=pk(lb))
                            tr.mm(bank(lb), onesf[:], ACP[par][:], False, True, r=["onesf", ("ACP", par)], w=pk(lb))
                            tr.recip(RL[:], bank(lb), r=pk(lb), w=["RL"])
                            tr.tt("dve", TA[1][:], bank(ob), RL[:], ALU.mult, r=pk(ob) + ["RL"], w=[("TA", 1)])
                            tr.tt("pool", GATE[:, h, qb * 512:(qb + 1) * 512], GATE[:, h, qb * 512:(qb + 1) * 512], TA[1][:], ALU.mult,
                                  r=[("GATE", h, qb), ("TA", 1)], w=[("GATE", h, qb)])
                        deferred.append((i + 3, epi))

                for t in proj_tasks(0):
                    t()
                LOOK = 2
                pending = []
                for i in range(min(LOOK, len(steps))):
                    emit_S(i)
                for i in range(len(steps)):
                    h, qb, kt = steps[i]
                    if qb == 0 and kt == 0 and h < 7:
                        pending = proj_tasks(h + 1)
                    if i + LOOK < len(steps):
                        emit_S(i + LOOK)
                    emit_PV(i)
                    while deferred and deferred[0][0] <= i:
                        deferred.pop(0)[1]()
                    if pending and (i % 2 == 1):
                        pending.pop(0)()
                assert not pending
                while deferred:
                    deferred.pop(0)[1]()

                tr.flush()
                ses.close()
                ses = ExitStack()
                XT = [ssb("XTd%d" % i, [128, D], F32) for i in range(3)]
                HNR = [ssb("HNRd%d" % i, [128, D], BF16) for i in range(3)]
                junk = ssb("junkd", [128, D], BF16)

                def d_front(i):
                    sl = i % 3
                    tr.dma("sp", XT[sl][:], A["x"][b, i * 128:(i + 1) * 128, :], w=[("XT", sl)])
                    for hf in range(2):
                        bk = 2 * sl + hf
                        for kc in range(8):
                            tr.mm(bank(bk), GATE[:, kc, i * 128:(i + 1) * 128], WO[:, kc, hf * 512:(hf + 1) * 512], kc == 0, kc == 7,
                                  r=["WO"] + [("GATE", kc, i // 4)], w=pk(bk))
                    tr.tt("dve", XT[sl][:], bank(2 * sl, 2), XT[sl][:], ALU.add, r=pk(2 * sl, 2) + [("XT", sl)], w=[("XT", sl)])
                    tr.dma("act", h1s[b, i * 128:(i + 1) * 128, :], XT[sl][:], r=[("XT", sl)], w=[("h1s", i)])
                    norm_part(XT[sl], ("XT", sl), sl)

                for i in range(NT + 2):
                    if i < NT:
                        d_front(i)
                    if i >= 2:
                        tr_part(i - 2, (i - 2) % 3, 6)
                tr.flush()
                ses.close()

            with ExitStack() as les:
                lsb = lambda name, shape, dt: les.enter_context(nc.sbuf_tensor(uq(name), shape, dt))
                UT = lsb("UT", [128, 16, 1024], BF16)
                ZG = lsb("ZG", [128, 8, T], BF16)
                WO = lsb("WO1", [128, 8, 1024], BF16)
                hnt_all = [("HNT", i) for i in range(NT)]
                bes = ExitStack()
                bsb = lambda name, shape, dt: bes.enter_context(nc.sbuf_tensor(uq(name), shape, dt))
                W0 = [bsb("W1_%d" % i, [128, 8, 512], BF16) for i in range(4)]
                T1 = [bsb("T1_%d" % i, [128, T], F32) for i in range(2)]
                XC = [bsb("XC%d" % i, [128, T], BF16) for i in range(2)]
                SG = bsb("SG", [128, T], BF16)
                XTM4 = bsb("XTM4", [128, 16, 4, 128], BF16)

                w1_slot = {}
                w1_n = [0]

                def w1_load(g):
                    if g not in w1_slot:
                        sl = w1_n[0] % 4
                        w1_n[0] += 1
                        tr.dma("sp", W0[sl][:], w1s[:, :, g * 512:(g + 1) * 512], w=[("W0", sl)])
                        w1_slot[g] = sl
                    return w1_slot[g]

                def inproj1(fc, n, pbase):
                    sl = w1_load(fc // 4)
                    off = (fc % 4) * 128
                    for tb in range(4):
                        bk = pbase + tb
                        for kc in range(8):
                            tr.mm(bank(bk), W0[sl][:, kc, off:off + 128], HNT[:, kc, tb * 512:(tb + 1) * 512], kc == 0, kc == 7,
                                  r=[("W0", sl)], w=pk(bk))

                def shortconv(fc, n, pbase):
                    s = n % 2
                    u = ps[:, pbase * 512:pbase * 512 + T]
                    kk = pk(pbase, 4)
                    tr.act(T1[s][:], u, AF.Identity, scale=convw[:, 1, fc:fc + 1], bias=convb[:, fc:fc + 1],
                           r=kk + [("convw", 1), "convb"], w=[("T1", s)])
                    tr.stt("dve", T1[s][:, 1:T], u[:, 0:T - 1], convw[:, 0, fc:fc + 1], T1[s][:, 1:T], ALU.mult, ALU.add,
                           r=kk + [("T1", s), ("convw", 0)], w=[("T1", s)])
                    tr.stt("dve", XC[s][:, 0:T - 1], u[:, 1:T], convw[:, 2, fc:fc + 1], T1[s][:, 0:T - 1], ALU.mult, ALU.add,
                           r=kk + [("T1", s), ("convw", 2)], w=[("XC", s)])
                    tr.copy("pool", XC[s][:, T - 1:T], T1[s][:, T - 1:T], r=[("T1", s)], w=[("XC", s)])

                chunks = [("x1", c, c) for c in range(8)] + [("v", c, 16 + c) for c in range(8)]

                def transposes(n):
                    which, c, fc = chunks[n]
                    s = n % 2
                    pb = 4 * s
                    for i in range(NT):
                        tr.mm(ps[:, pb * 512 + i * 128: pb * 512 + (i + 1) * 128], XC[s][:, i * 128:(i + 1) * 128], ident[:], True, True,
                              r=[("XC", s), "ident"], w=pk(pb, 4))
                    src = bank(pb, 4).rearrange("p (a n) -> p a n", n=128)
                    if which == "v":
                        tr.copy("act", UT[:, :, c * 128:(c + 1) * 128], src, r=pk(pb, 4), w=[("UT", c // 4)])
                    else:
                        tr.copy("act", XTM4[:, :, c % 4, :], src, r=pk(pb, 4), w=["XTM4"])
                        if c % 4 == 3:
                            cbx = c // 4
                            tr.dma("act", x1t[b, :, cbx * 512:(cbx + 1) * 512].rearrange("(a p) (c n) -> p a c n", p=128, n=128),
                                   XTM4[:], r=["XTM4"], w=["x1t"])

                for n in range(len(chunks)):
                    which, c, fc = chunks[n]
                    pbase = 4 * (n % 2)
                    inproj1(fc, n, pbase)
                    if n >= 1:
                        transposes(n - 1)
                    shortconv(fc, n, pbase)
                transposes(len(chunks) - 1)
                n = len(chunks)
                for c in range(8):
                    inproj1(24 + c, n, 0)
                    tr.act(SG[:], ps[:, 0:T], AF.Silu, r=pk(0, 4), w=["SG"])
                    n += 1
                    inproj1(8 + c, n, 4)
                    shortconv(8 + c, c, 4)
                    s = c % 2
                    tr.tt("pool", XC[s][:], XC[s][:], SG[:], ALU.mult, r=[("XC", s), "SG"], w=[("XC", s)])
                    tr.dma("act", xgs[b, c * 128:(c + 1) * 128, :], XC[s][:], r=[("XC", s)], w=[("xgs", c)])
                    n += 1

                tr.flush()
                bes.close()
                bes = ExitStack()
                Z = bsb("Z", [128, 16, 2, 512], BF16)
                FC = [bsb("FC%d" % i, [128, 16, 128], BF16) for i in range(2)]
                FS = [bsb("FS%d" % i, [128, 16, 128], BF16) for i in range(2)]
                KF = [bsb("KF%d" % i, [128, 2, 512], F32) for i in range(2)]
                M = [bsb("M%d" % i, [128, 512], F32) for i in range(4)]
                X1 = [bsb("X1_%d" % i, [128, 512], BF16) for i in range(2)]
                Y2 = [bsb("Y2_%d" % i, [128, 512], BF16) for i in range(2)]
                XG = bsb("XG", [128, 4, T], BF16)
                nf = 0
                for cb in range(2):
                    utk = [("UT", cb)]
                    for order in range(2):
                        for j in range(16):
                            s = nf % 2
                            nf += 1
                            tr.dma("sp", FC[s][:], A["c_c2f"][j], w=[("FC", s)])
                            tr.dma("sp", FS[s][:], A["c_s2f"][j], w=[("FS", s)])
                            tr.dma("sp", KF[s][:], kfs[order, j, :, :, cb * 512:(cb + 1) * 512], w=[("KF", s)])
                            ba, bb = 2 * s, 2 * s + 1
                            for sc in range(16):
                                tr.mm(bank(ba), FC[s][:, sc, :], UT[:, sc, cb * 512:(cb + 1) * 512], sc == 0, sc == 15,
                                      r=[("FC", s)] + utk, w=pk(ba))
                            for sc in range(16):
                                tr.mm(bank(bb), FS[s][:, sc, :], UT[:, sc, cb * 512:(cb + 1) * 512], sc == 0, sc == 15,
                                      r=[("FS", s)] + utk, w=pk(bb))
                            kr, ki = KF[s][:, 0, :], KF[s][:, 1, :]
                            tr.tt("dve", M[0][:], bank(ba), kr, ALU.mult, r=pk(ba) + [("KF", s)], w=[("M", 0)])
                            tr.tt("dve", M[1][:], bank(bb), ki, ALU.mult, r=pk(bb) + [("KF", s)], w=[("M", 1)])
                            tr.tt("dve", M[2][:], bank(bb), kr, ALU.mult, r=pk(bb) + [("KF", s)], w=[("M", 2)])
                            tr.tt("dve", M[3][:], bank(ba), ki, ALU.mult, r=pk(ba) + [("KF", s)], w=[("M", 3)])
                            tr.tt("pool", Z[:, j, 0, :], M[0][:], M[1][:], ALU.add, r=[("M", 0), ("M", 1)], w=[("Z", j)])
                            tr.tt("pool", Z[:, j, 1, :], M[2][:], M[3][:], ALU.subtract, r=[("M", 2), ("M", 3)], w=[("Z", j)])
                        if order == 0:
                            for cc in range(4):
                                tr.dma("sp", XG[:, cc, :], xgs[b, (cb * 4 + cc) * 128:(cb * 4 + cc + 1) * 128, :],
                                       r=[("xgs", cb * 4 + cc)], w=[("XG", cc)])
                            if cb == 0:
                                tr.dma("sp", WO[:], wo1s, w=["WO"])
                        zk = [("Z", j) for j in range(16)]
                        for tc in range(16):
                            s = nf % 2
                            nf += 1
                            tr.dma("sp", FC[s][:], A["c_c2f"][tc], w=[("FC", s)])
                            tr.dma("sp", FS[s][:], A["c_s2f"][tc], w=[("FS", s)])
                            bo = 4 + s
                            for fcn in range(16):
                                tr.mm(bank(bo), FC[s][:, fcn, :], Z[:, fcn, 0, :], fcn == 0, False, r=[("FC", s)] + zk, w=pk(bo), inc=False)
                                tr.mm(bank(bo), FS[s][:, fcn, :], Z[:, fcn, 1, :], False, fcn == 15, r=[("FS", s)] + zk, w=pk(bo), inc=(fcn == 15))
                            if order == 0:
                                tr.dma("sp", X1[s][:], x1t[b, tc * 128:(tc + 1) * 128, cb * 512:(cb + 1) * 512], r=["x1t"], w=[("X1", s)])
                                tr.tt("dve", UT[:, tc, cb * 512:(cb + 1) * 512], bank(bo), X1[s][:], ALU.mult,
                                      r=pk(bo) + [("X1", s)], w=utk)
                            else:
                                tr.copy("act", Y2[s][:], bank(bo), r=pk(bo), w=[("Y2", s)])
                                bt_ = 6 + s
                                for cc in range(4):
                                    tr.mm(ps[:, bt_ * 512 + cc * 128: bt_ * 512 + (cc + 1) * 128], Y2[s][:, cc * 128:(cc + 1) * 128], ident[:],
                                          True, True, r=[("Y2", s), "ident"], w=pk(bt_), inc=(cc == 3))
                                tr.tt("dve", ZG[:, cb * 4:(cb + 1) * 4, tc * 128:(tc + 1) * 128],
                                      bank(bt_).rearrange("p (a n) -> p a n", n=128), XG[:, :, tc * 128:(tc + 1) * 128], ALU.mult,
                                      r=pk(bt_) + [("XG", cc) for cc in range(4)], w=[("ZG", cb, tc)])

                tr.flush()
                bes.close()
                bes = ExitStack()
                H1T = [bsb("H1L%d" % i, [128, D], F32) for i in range(3)]
                OT = [bsb("OT%d" % i, [128, D], F32) for i in range(3)]
                junk = bsb("junk1", [128, D], BF16)
                st = bsb("st1", [128, 16], F32)
                nxt = b + 1 < nseq
                if nxt:
                    XTn = [bsb("XTn%d" % i, [128, D], F32) for i in range(3)]
                    HNRn = [bsb("HNRn%d" % i, [128, D], BF16) for i in range(3)]
                    junkn = bsb("junkn", [128, D], BF16)
                    stn = bsb("stn", [128, 16], F32)
                for i in range(NT + 1):
                    if i < NT:
                        s = i % 3
                        c0 = 3 * s
                        sk = ("st", s)
                        tr.dma("sp", H1T[s][:], h1s[b, i * 128:(i + 1) * 128, :], r=[("h1s", i)], w=[("H1T", s)])
                        if nxt:
                            tr.dma("sp", XTn[s][:], A["x"][b + 1, i * 128:(i + 1) * 128, :], w=[("XTn", s)])
                        for hf in range(2):
                            bk = 2 * s + hf
                            for kc in range(8):
                                tr.mm(bank(bk), ZG[:, kc, i * 128:(i + 1) * 128], WO[:, kc, hf * 512:(hf + 1) * 512], kc == 0, kc == 7,
                                      r=["WO", ("ZG", kc // 4, i)], w=pk(bk))
                        tr.tt("dve", H1T[s][:], bank(2 * s, 2), H1T[s][:], ALU.add, r=pk(2 * s, 2) + [("H1T", s)], w=[("H1T", s)])
                        tr.act(junk[:], H1T[s][:], AF.Square, accum=st[:, c0:c0 + 1], r=[("H1T", s)], w=[sk])
                        tr.act(st[:, c0 + 1:c0 + 2], st[:, c0:c0 + 1], AF.Sqrt, scale=1.0 / D, bias=epst[:, 0:1], r=[sk, "eps"], w=[sk])
                        tr.recip(st[:, c0 + 2:c0 + 3], st[:, c0 + 1:c0 + 2], r=[sk], w=[sk])
                        tr.stt("dve", OT[s][:], H1T[s][:], st[:, c0 + 2:c0 + 3], gfin[:], ALU.mult, ALU.mult,
                               r=[("H1T", s), sk, "gfin"], w=[("OT", s)])
                        tr.dma("act", out_ap[b, i * 128:(i + 1) * 128, :], OT[s][:], r=[("OT", s)], w=[("out", i)])
                        if nxt:
                            norm_part_g(XTn[s], ("XTn", s), s, HNRn, stn, junkn, "n")
                    if nxt and i >= 1:
                        tr_part_g(i - 1, (i - 1) % 3, 6, HNRn, "n")
                tr.flush()
                bes.close()
    return nc


_NC_CACHE = {}


def _run(inputs, nseq_per_core, ncores, dbg=False):
    key = (nseq_per_core, dbg)
    if key not in _NC_CACHE:
        _NC_CACHE[key] = build_program(nseq_per_core, dbg)
    nc = _NC_CACHE[key]
    cs = _consts()
    x = np.ascontiguousarray(np.asarray(inputs["x"], dtype=np.float32))
    in_maps = []
    for c in range(ncores):
        m = {"x": np.ascontiguousarray(x[c * nseq_per_core:(c + 1) * nseq_per_core])}
        for name, _ in PARAMS:
            m[name] = np.ascontiguousarray(np.asarray(inputs[name], dtype=np.float32))
        m.update(cs)
        in_maps.append(m)
    res = run_bass_kernel_spmd(nc, in_maps, core_ids=list(range(ncores)))
    return res


def kernel(**inputs):
    res = _run(inputs, SEQ_PER_CORE, NCORES)
    return np.concatenate([r["out"] for r in res.results], axis=0)
```
